# Optimizing a Trainium2 kernel written in Bass

```python
import jax, jax.numpy as jnp
from jax import lax
import numpy as np

D_MODEL = 1024
BATCH = 8
SEQ = 8192
DEPTH = 1

N_HEADS = 8
QK_NOPE_DIM = 64
QK_ROPE_DIM = 32
V_HEAD_DIM = 64
Q_LORA_RANK = 256
KV_LORA_RANK = 128
ATTN_WIDTH = N_HEADS * V_HEAD_DIM
CONV_WIDTH = 512
CONV_K = 3
N_BRANCH = 2
BRANCH_WIDTH = 512
D_FF = 2816
N_SUB = 3
ROPE_THETA = 10000.0
Q_BLOCK = 128
EPS = 1e-6
IN_COLS = Q_LORA_RANK + KV_LORA_RANK + QK_ROPE_DIM + 3 * CONV_WIDTH + N_BRANCH * D_MODEL

kernel_name = "hybrid_mla_shortconv_macaron_adaln"


def rms_norm(x, w):
    xf = x.astype(jnp.float32)
    y = xf * lax.rsqrt(jnp.mean(xf * xf, axis=-1, keepdims=True) + EPS)
    return (y * w.astype(jnp.float32)).astype(x.dtype)


def modulate(x, w, shift, scale):
    return rms_norm(x, w) * (1 + scale[:, None, :]) + shift[:, None, :]


def swiglu(h, w13, w2):
    gu = jnp.einsum('bsd,df->bsf', h, w13)
    g, u = jnp.split(gu, 2, axis=-1)
    return jnp.einsum('bsf,fd->bsd', jax.nn.silu(g) * u, w2)


def rope_tables(positions):
    inv_freq = 1.0 / (ROPE_THETA ** (jnp.arange(0, QK_ROPE_DIM, 2, dtype=jnp.float32) / QK_ROPE_DIM))
    ang = positions.astype(jnp.float32)[..., None] * inv_freq
    return jnp.cos(ang), jnp.sin(ang)


def apply_rope(x, cos, sin):
    xf = x.astype(jnp.float32)
    x1, x2 = jnp.split(xf, 2, axis=-1)
    return jnp.concatenate([x1 * cos - x2 * sin, x2 * cos + x1 * sin], axis=-1).astype(x.dtype)


def causal_mla_attention(q_nope, q_rope, k_nope, k_rope, v):
    b, s, h, _ = q_nope.shape
    nb = s // Q_BLOCK
    scale = (QK_NOPE_DIM + QK_ROPE_DIM) ** -0.5
    key_idx = jnp.arange(s)

    def to_blocks(t):
        return jnp.moveaxis(t.reshape(b, nb, Q_BLOCK, *t.shape[2:]), 1, 0)

    def block(args):
        qn, qr, i = args
        sc = (jnp.einsum('bqhd,bkhd->bhqk', qn, k_nope, preferred_element_type=jnp.float32)
              + jnp.einsum('bqhr,bkr->bhqk', qr, k_rope, preferred_element_type=jnp.float32))
        q_idx = i * Q_BLOCK + jnp.arange(Q_BLOCK)
        mask = key_idx[None, :] <= q_idx[:, None]
        sc = jnp.where(mask, sc * scale, -jnp.inf)
        p = jax.nn.softmax(sc, axis=-1).astype(v.dtype)
        return jnp.einsum('bhqk,bkhd->bqhd', p, v)

    out = lax.map(block, (to_blocks(q_nope), to_blocks(q_rope), jnp.arange(nb)))
    return jnp.moveaxis(out, 0, 1).reshape(b, s, h * V_HEAD_DIM)


def hybrid_mixer(h, cos, sin, w_in, q_a_norm, w_uq, kv_a_norm, w_ukv,
                 q_norm_nope, k_norm_nope, q_norm_rope, k_norm_rope,
                 conv_w, w_branch, w_out):
    b, s, _ = h.shape
    proj = jnp.einsum('bsd,dn->bsn', h, w_in)
    i1 = Q_LORA_RANK
    i2 = i1 + KV_LORA_RANK
    i3 = i2 + QK_ROPE_DIM
    i4 = i3 + 3 * CONV_WIDTH
    c_q, c_kv, k_rope, conv_in, gate_logits = jnp.split(proj, [i1, i2, i3, i4], axis=-1)

    q = jnp.einsum('bsr,rn->bsn', rms_norm(c_q, q_a_norm), w_uq)
    q = q.reshape(b, s, N_HEADS, QK_NOPE_DIM + QK_ROPE_DIM)
    q_nope, q_rope = q[..., :QK_NOPE_DIM], q[..., QK_NOPE_DIM:]
    kv = jnp.einsum('bsr,rn->bsn', rms_norm(c_kv, kv_a_norm), w_ukv)
    kv = kv.reshape(b, s, N_HEADS, QK_NOPE_DIM + V_HEAD_DIM)
    k_nope, v = kv[..., :QK_NOPE_DIM], kv[..., QK_NOPE_DIM:]
    q_nope = rms_norm(q_nope, q_norm_nope)
    k_nope = rms_norm(k_nope, k_norm_nope)
    q_rope = apply_rope(rms_norm(q_rope, q_norm_rope), cos[:, :, None, :], sin[:, :, None, :])
    k_rope = apply_rope(rms_norm(k_rope, k_norm_rope), cos, sin)
    o_attn = causal_mla_attention(q_nope, q_rope, k_nope, k_rope, v)

    xv, gate_b, gate_c = jnp.split(conv_in, 3, axis=-1)
    u = gate_c * xv
    u = lax.conv_general_dilated(u, conv_w[:, None, :], window_strides=(1,),
                                 padding=[(CONV_K - 1, 0)],
                                 dimension_numbers=('NWC', 'WIO', 'NWC'),
                                 feature_group_count=CONV_WIDTH)
    o_conv = gate_b * u

    branches = jnp.stack([o_attn, o_conv], axis=2)
    y_br = jnp.einsum('bsgk,gkd->bsgd', branches, w_branch)
    gates = jax.nn.sigmoid(gate_logits.reshape(b, s, N_BRANCH, D_MODEL))
    merged = jnp.sum(gates * y_br, axis=2)
    return jnp.einsum('bsd,de->bse', merged, w_out)


def setup_inputs(seed: int = 0) -> dict:
    key = jax.random.key(seed)
    ks = jax.random.split(key, 20)
    f32 = jnp.float32

    def nrm(k, shape, fan_in):
        return jax.random.normal(k, shape, f32) * (fan_in ** -0.5)

    def gain(k, shape):
        return 1.0 + 0.02 * jax.random.normal(k, shape, f32)

    x = jax.random.normal(ks[0], (BATCH, SEQ, D_MODEL), f32)
    c = jax.random.normal(ks[1], (BATCH, D_MODEL), f32)
    offsets = jax.random.randint(ks[2], (BATCH, 1), 0, 1024, dtype=jnp.int32)
    positions = jnp.arange(SEQ, dtype=jnp.int32)[None, :] + offsets
    return {
        "x": x,
        "c": c,
        "positions": positions,
        "w_ada": nrm(ks[3], (DEPTH, D_MODEL, 3 * N_SUB * D_MODEL), D_MODEL),
        "b_ada": 0.01 * jax.random.normal(ks[4], (DEPTH, 3 * N_SUB * D_MODEL), f32),
        "norm_w": gain(ks[5], (DEPTH, N_SUB, D_MODEL)),
        "ffn_w13": nrm(ks[6], (DEPTH, 2, D_MODEL, 2 * D_FF), D_MODEL),
        "ffn_w2": nrm(ks[7], (DEPTH, 2, D_FF, D_MODEL), D_FF),
        "w_in": nrm(ks[8], (DEPTH, D_MODEL, IN_COLS), D_MODEL),
        "q_a_norm": gain(ks[9], (DEPTH, Q_LORA_RANK)),
        "w_uq": nrm(ks[10], (DEPTH, Q_LORA_RANK, N_HEADS * (QK_NOPE_DIM + QK_ROPE_DIM)), Q_LORA_RANK),
        "kv_a_norm": gain(ks[11], (DEPTH, KV_LORA_RANK)),
        "w_ukv": nrm(ks[12], (DEPTH, KV_LORA_RANK, N_HEADS * (QK_NOPE_DIM + V_HEAD_DIM)), KV_LORA_RANK),
        "q_norm_nope": gain(ks[13], (DEPTH, QK_NOPE_DIM)),
        "k_norm_nope": gain(ks[14], (DEPTH, QK_NOPE_DIM)),
        "q_norm_rope": gain(ks[15], (DEPTH, QK_ROPE_DIM)),
        "k_norm_rope": gain(ks[16], (DEPTH, QK_ROPE_DIM)),
        "conv_w": nrm(ks[17], (DEPTH, CONV_K, CONV_WIDTH), CONV_K),
        "w_branch": nrm(ks[18], (DEPTH, N_BRANCH, BRANCH_WIDTH, D_MODEL), BRANCH_WIDTH),
        "w_out": nrm(ks[19], (DEPTH, D_MODEL, D_MODEL), D_MODEL),
    }


def reference(x, c, positions, w_ada, b_ada, norm_w, ffn_w13, ffn_w2, w_in,
              q_a_norm, w_uq, kv_a_norm, w_ukv, q_norm_nope, k_norm_nope,
              q_norm_rope, k_norm_rope, conv_w, w_branch, w_out):
    b = x.shape[0]
    cos, sin = rope_tables(positions)
    c_act = jax.nn.silu(c)
    for l in range(DEPTH):
        mod = (jnp.einsum('bd,dn->bn', c_act, w_ada[l]) + b_ada[l]).reshape(b, N_SUB, 3, D_MODEL)
        h = modulate(x, norm_w[l, 0], mod[:, 0, 0], mod[:, 0, 1])
        x = x + 0.5 * mod[:, 0, 2][:, None, :] * swiglu(h, ffn_w13[l, 0], ffn_w2[l, 0])
        h = modulate(x, norm_w[l, 1], mod[:, 1, 0], mod[:, 1, 1])
        y = hybrid_mixer(h, cos, sin, w_in[l], q_a_norm[l], w_uq[l], kv_a_norm[l], w_ukv[l],
                         q_norm_nope[l], k_norm_nope[l], q_norm_rope[l], k_norm_rope[l],
                         conv_w[l], w_branch[l], w_out[l])
        x = x + mod[:, 1, 2][:, None, :] * y
        h = modulate(x, norm_w[l, 2], mod[:, 2, 0], mod[:, 2, 1])
        x = x + 0.5 * mod[:, 2, 2][:, None, :] * swiglu(h, ffn_w13[l, 1], ffn_w2[l, 1])
    return x
```

```python
import math
from contextlib import ExitStack

import numpy as np
import ml_dtypes
import concourse.bass as bass
import concourse.mybir as mybir
from concourse.bass_utils import run_bass_kernel_spmd

F32 = mybir.dt.float32
BF16 = mybir.dt.bfloat16
I32 = mybir.dt.int32
ALU = mybir.AluOpType
AF = mybir.ActivationFunctionType

D = 1024
DFF = 2816
NF = DFF // 128
H = 8
EPS = 1e-6
TT = 512
NDMA_SEMS = 20


class Op:
    __slots__ = ("eng", "fn", "deps", "signal", "dma", "sem", "val", "idx", "cost", "seg", "sched",
                 "start", "fin", "pos", "sync", "bar")


DEF_COST = {"pe": 260.0, "act": 620.0, "dve": 620.0, "pool": 1300.0, "sp": 60.0}
SEM_LAT = 200.0
DMA_BW = 170.0
DMA_LAT = 1900.0
WINDOW = 160


class Prog:
    def __init__(self, nc, st):
        self.nc = nc
        self.ops = []
        self.lw = {}
        self.rd = {}
        self.seg = 0
        self.seg_ops = [[]]
        self.engs = {"pe": nc.tensor, "act": nc.scalar, "dve": nc.vector,
                     "pool": nc.gpsimd, "sp": nc.sync}
        self.esem = {e: st.enter_context(nc.semaphore("es_" + e)) for e in self.engs}
        self.dsem = {e: [st.enter_context(nc.semaphore("ds_%s_%d" % (e, i)))
                         for i in range(NDMA_SEMS)] for e in ("sp", "pool", "act")}
        self.do_sched = True

    def _new(self, eng, fn, dma, cost):
        op = Op()
        op.eng = eng; op.fn = fn; op.dma = dma; op.signal = False
        op.sem = None; op.val = 0; op.idx = len(self.ops); op.cost = cost
        op.seg = self.seg; op.sched = False; op.start = 0.0; op.fin = 0.0; op.pos = 0
        op.sync = None; op.bar = False; op.deps = {}
        return op

    def _add(self, eng, fn, reads, writes, dma, cost):
        op = self._new(eng, fn, dma, cost)
        deps = op.deps
        for k in reads:
            w = self.lw.get(k)
            if w is not None and w is not op:
                deps[id(w)] = (w, True)
            if type(k) is tuple and k[0] == "ps":
                for d in self.rd.get(k, ()):
                    if d.eng != eng and id(d) not in deps:
                        deps[id(d)] = (d, True)
        for k in writes:
            w = self.lw.get(k)
            if w is not None and w is not op and id(w) not in deps:
                deps[id(w)] = (w, False)
            for d in self.rd.get(k, ()):
                if d is not op and id(d) not in deps:
                    deps[id(d)] = (d, False)
        for k in reads:
            r = self.rd.get(k)
            if r is None:
                r = self.rd[k] = []
            r.append(op)
        for k in writes:
            self.lw[k] = op
            self.rd[k] = []
        self.ops.append(op)
        self.seg_ops[-1].append(op)
        return op

    def op(self, eng, fn, reads=(), writes=(), cost=None):
        return self._add(eng, fn, reads, writes, False, DEF_COST[eng] if cost is None else cost)

    def dma(self, eng, out, in_, reads=(), writes=(), nbytes=None):
        E = self.engs[eng]
        if nbytes is None:
            try:
                n = 1
                for d_ in out.shape:
                    n *= int(d_)
                nbytes = n * (2 if out.dtype == BF16 else 4)
            except Exception:
                nbytes = 262144
        return self._add(eng, lambda: E.dma_start(out=out, in_=in_), reads, writes, True, float(nbytes))

    def barrier(self):
        seg = self.seg_ops[-1]
        prev = None
        bars = []
        for e in ("sp", "pool", "act", "dve", "pe"):
            E = self.engs[e]
            op = self._new(e, (lambda E=E: E.nop()), False, 50.0)
            op.bar = True
            if prev is None:
                op.deps = {id(d): (d, True) for d in seg}
            else:
                op.deps = {id(prev): (prev, True)}
            self.ops.append(op)
            bars.append(op)
            prev = op
        self.seg_ops.append(bars)
        self.seg += 1
        self.seg_ops.append([])
        self.seg += 1
        for op in bars:
            op.seg = self.seg - 1
        self.lw = {}
        self.rd = {}

    def _schedule_segment(self, ops, t0):
        if not ops:
            return [], t0
        if ops[0].bar or not self.do_sched:
            t = t0
            for op in ops:
                op.sched = True
                op.start = t
                t += 1.0
                op.fin = t
            return list(ops), t
        queues = {e: [] for e in self.engs}
        for op in ops:
            queues[op.eng].append(op)
        head = {e: 0 for e in self.engs}
        free = {e: t0 for e in self.engs}
        dma_free = t0
        left = len(ops)
        out = []
        tmax = t0
        while left:
            best = None
            for e, q in queues.items():
                h = head[e]
                nq = len(q)
                while h < nq and q[h].sched:
                    h += 1
                head[e] = h
                if h >= nq:
                    continue
                fe = free[e]
                cand = None
                cnt = 0
                i = h
                while i < nq and cnt < WINDOW:
                    op = q[i]
                    i += 1
                    if op.sched:
                        continue
                    cnt += 1
                    ok = True
                    rt = t0
                    for d, raw in op.deps.values():
                        if not d.sched:
                            ok = False
                            break
                        f = d.fin
                        if d.dma or d.eng != e:
                            f += SEM_LAT
                        if f > rt:
                            rt = f
                    if not ok:
                        continue
                    stt = rt if rt > fe else fe
                    if cand is None or stt < cand[0]:
                        cand = (stt, op)
                    if stt <= fe:
                        break
                if cand is not None and (best is None or cand[0] < best[0]):
                    best = (cand[0], cand[1], e)
            assert best is not None, "scheduler stuck"
            stt, op, e = best
            op.sched = True
            op.start = stt
            if op.dma:
                free[e] = stt + DEF_COST["sp"]
                tb_ = stt if stt > dma_free else dma_free
                dma_free = tb_ + op.cost / DMA_BW
                op.fin = dma_free + DMA_LAT
            else:
                op.fin = stt + op.cost
                free[e] = op.fin
            if op.fin > tmax:
                tmax = op.fin
            out.append(op)
            left -= 1
        out.sort(key=lambda o: (o.start, o.idx))
        return out, tmax

    def emit(self):
        order = []
        t = 0.0
        for seg in self.seg_ops:
            o, t = self._schedule_segment(seg, t)
            order.extend(o)
        self.est_ns = t
        pos = {e: 0 for e in self.engs}
        for op in order:
            op.pos = pos[op.eng]
            pos[op.eng] += 1
        for op in order:
            best = {}
            sync = []
            for d, raw in op.deps.values():
                if d.dma:
                    sync.append(d)
                elif op.dma or d.eng != op.eng or (raw and op.eng != "pe"):
                    b = best.get(d.eng)
                    if b is None or d.pos > b.pos:
                        best[d.eng] = d
            sync.extend(best.values())
            for d in sync:
                d.signal = True
            op.sync = sync
        cnt = {e: 0 for e in self.engs}
        waited = {e: {} for e in self.engs}
        rr = {e: 0 for e in self.engs}
        dcnt = {}
        for op in order:
            E = self.engs[op.eng]
            need = {}
            for d in op.sync:
                s = d.sem
                assert s is not None, "dep without sem"
                key = id(s)
                if key not in need or need[key][1] < d.val:
                    need[key] = (s, d.val)
            if op.dma and op.signal:
                sems = self.dsem[op.eng]
                s = sems[rr[op.eng] % len(sems)]
                rr[op.eng] += 1
                prev = dcnt.get(id(s), 0)
                if prev > 0:
                    key = id(s)
                    if key not in need or need[key][1] < prev:
                        need[key] = (s, prev)
                op.sem = s
                op.val = prev + 16
                dcnt[id(s)] = op.val
            w = waited[op.eng]
            for key, (s, v) in need.items():
                if w.get(key, 0) < v:
                    E.wait_ge(s, v)
                    w[key] = v
            ins = op.fn()
            if op.signal:
                if op.dma:
                    ins.then_inc(op.sem, 16)
                else:
                    cnt[op.eng] += 1
                    op.sem = self.esem[op.eng]
                    op.val = cnt[op.eng]
                    ins.then_inc(op.sem, 1)


class K:
    pass


def build(S, debug_outs=()):
    NT = S // TT
    KT = S // 128
    nc = bass.Bass("TRN2", target_bir_lowering=False)
    st = ExitStack()
    P = Prog(nc, st)
    pe, act, dve, pool = nc.tensor, nc.scalar, nc.vector, nc.gpsimd

    def din(name, shape, dt=F32):
        return nc.dram_tensor(name, list(shape), dt, kind="ExternalInput").ap()

    def dscr(name, shape, dt):
        kind = "ExternalOutput" if name in debug_outs else "Internal"
        return nc.dram_tensor(name, list(shape), dt, kind=kind).ap()

    xT = din("xT", [D, S])
    c_in = din("c_in", [128, 8])
    pos_in = din("pos_in", [96, S], I32)
    invf = din("invf", [96, 1])
    w_ada = din("w_ada", [D, 9 * D])
    b_ada = din("b_ada", [128, 72])
    normw = din("normw", [128, 24])
    w13 = din("w13", [2, D, 2 * DFF])
    w2 = din("w2", [2, DFF, D])
    w_in = din("w_in", [D, 4000])
    qa = din("qa", [128, 2])
    kva = din("kva", [128, 1])
    w_uq = din("w_uq", [256, 768])
    w_ukv = din("w_ukv", [128, 1024])
    qgain = din("qgain", [96, 1])
    kgain = din("kgain", [128, 1])
    krgain = din("krgain", [32, 1])
    convw = din("convw", [128, 12])
    w_br = din("w_br", [2, 512, D])
    w_out = din("w_out", [D, D])
    cmat = din("cmat", [128, 7 * 128])
    outT = nc.dram_tensor("outT", [D, S], F32, kind="ExternalOutput").ap()

    x1s = dscr("x1s", [D, S], F32)
    x2s = dscr("x2s", [D, S], F32)
    cos_s = dscr("cos_s", [96, S], F32)
    sin_s = dscr("sin_s", [96, S], F32)
    q_s = dscr("q_s", [H, 96, S], BF16)
    kn_s = dscr("kn_s", [4, 128, S], BF16)
    kr_s = dscr("kr_s", [32, S], BF16)
    v_s = dscr("v_s", [KT, 128, H * 65], BF16)
    o_s = dscr("o_s", [H, 64, S], BF16)
    g0_s = dscr("g0_s", [D, S], BF16)
    yc_s = dscr("yc_s", [D, S], BF16)

    def sb(stk, name, shape, dt):
        return stk.enter_context(nc.sbuf_tensor(name, list(shape), dt))

    PS = [st.enter_context(nc.psum_tensor("ps%d" % i, [128, 512], F32)) for i in range(8)]

    def pk(i):
        return ("ps", i)

    cm = sb(st, "cm", [128, 7 * 128], BF16)
    cmf = sb(st, "cmf", [128, 128], F32)
    modv = sb(st, "modv", [128, 72], F32)
    Asc = sb(st, "Asc", [128, 24], F32)
    Gt = sb(st, "Gt", [128, 24], F32)
    nw = sb(st, "nw", [128, 24], F32)
    qa_t = sb(st, "qa_t", [128, 2], F32)
    kva_t = sb(st, "kva_t", [128, 1], F32)
    qg_t = sb(st, "qg_t", [96, 1], F32)
    kg_t = sb(st, "kg_t", [128, 1], F32)
    krg_t = sb(st, "krg_t", [32, 1], F32)
    cw_t = sb(st, "cw_t", [128, 12], F32)
    eps_t = sb(st, "eps_t", [128, 1], F32)
    negpi_t = sb(st, "negpi_t", [128, 1], F32)

    ONES = cm[:, 0:128]
    BO96 = cm[0:96, 128:128 + 96]
    BO128 = cm[:, 256:384]
    R96T = cm[0:96, 384:384 + 96]
    R32T = cm[0:32, 512:512 + 32]
    BO32 = cm[0:32, 512 + 32:512 + 64]
    TRI = cm[:, 640:768]
    IDENT = cm[:, 768:896]

    P.dma("pool", cm[:], cmat, writes=["cm"])
    P.op("dve", lambda: dve.memset(cmf[:], 1.0), writes=["cmf"])
    P.op("dve", lambda: dve.memset(eps_t[:], EPS), writes=["eps"])
    P.op("dve", lambda: dve.memset(negpi_t[:], -math.pi), writes=["negpi"])
    for t, src, key in ((nw, normw, "nw"), (qa_t, qa, "qa"), (kva_t, kva, "kva"), (qg_t, qgain, "qg"),
                        (kg_t, kgain, "kg"), (krg_t, krgain, "krg"), (cw_t, convw, "cw")):
        P.dma("sp", t[:], src, writes=[key])

    def ffn_w13(stk, l, pname):
        w13t = sb(stk, pname + "w13", [128, 8, 2 * DFF], BF16)
        w13v = w13[l].rearrange("(c p) n -> p c n", p=128)
        for kc in range(8):
            for hf in range(2):
                P.dma("pool", w13t[:, kc, hf * DFF:(hf + 1) * DFF], w13v[:, kc, hf * DFF:(hf + 1) * DFF],
                      writes=[("w13", kc)])
        return w13t

    def ffn_w2(stk, l, pname):
        w2t = sb(stk, pname + "w2", [128, NF, D], BF16)
        w2v = w2[l].rearrange("(c p) n -> p c n", p=128)
        for j in range(NF):
            P.dma("pool", w2t[:, j, :], w2v[:, j, :], writes=[("w2", j)])
        return w2t

    def ffn_weights(stk, l, pname):
        return ffn_w13(stk, l, pname), ffn_w2(stk, l, pname)

    wst = ExitStack()
    pre_ffn1 = ffn_weights(wst, 0, "fa")

    with ExitStack() as ph:
        c_t = sb(ph, "c_t", [128, 8], F32)
        ca_t = sb(ph, "ca_t", [128, 8], F32)
        b_t = sb(ph, "b_t", [128, 72], F32)
        NB = 18
        CB = 9 * D // NB
        wa = [sb(ph, "wa%d" % i, [128, 8, CB], F32) for i in range(2)]
        P.dma("sp", c_t[:], c_in, writes=["c"])
        P.dma("sp", b_t[:], b_ada, writes=["b"])
        P.op("act", lambda: act.activation(out=ca_t[:], in_=c_t[:], func=AF.Silu),
             reads=["c"], writes=["ca"])
        w_ada_v = w_ada.rearrange("(c p) n -> p c n", p=128)
        for blk in range(NB):
            slot = blk % 2
            for kc in range(8):
                P.dma("sp", wa[slot][:, kc, :], w_ada_v[:, kc, blk * CB:(blk + 1) * CB],
                      writes=[("wa", slot, kc)])
            for jj in range(CB // 128):
                j = blk * (CB // 128) + jj
                for kc in range(8):
                    P.op("pe", lambda slot=slot, kc=kc, jj=jj, j=j: pe.matmul(
                        PS[7][:, j:j + 1], wa[slot][:, kc, jj * 128:(jj + 1) * 128],
                        ca_t[:, kc:kc + 1], start=(kc == 0), stop=(kc == 7)),
                        reads=[("wa", slot, kc), "ca"], writes=[pk(7)], cost=230.0)
        P.op("dve", lambda: dve.tensor_tensor(out=modv[:], in0=PS[7][:, 0:72], in1=b_t[:], op=ALU.add),
             reads=[pk(7), "b"], writes=["modv"])
        for s in range(3):
            P.op("dve", lambda s=s: dve.scalar_tensor_tensor(
                out=Asc[:, s * 8:(s + 1) * 8], in0=modv[:, s * 24 + 8:s * 24 + 16], scalar=1.0,
                in1=nw[:, s * 8:(s + 1) * 8], op0=ALU.add, op1=ALU.mult),
                reads=["modv", "nw"], writes=[("Asc", s)])
            P.op("dve", lambda s=s: dve.tensor_scalar(
                out=Gt[:, s * 8:(s + 1) * 8], in0=modv[:, s * 24 + 16:s * 24 + 24],
                scalar1=(1.0 if s == 1 else 0.5), scalar2=None, op0=ALU.mult),
                reads=["modv"], writes=[("Gt", s)])

        RC = min(1024, S)
        pos_i = sb(ph, "pos_i", [96, RC], I32)
        ang = sb(ph, "ang", [96, RC], F32)
        tmp = sb(ph, "tmp", [96, RC], F32)
        tmpi = sb(ph, "tmpi", [96, RC], I32)
        tmp2 = sb(ph, "tmp2", [96, RC], F32)
        tab = [sb(ph, "tab%d" % i, [96, RC], F32) for i in range(2)]
        invf_t = sb(ph, "invf_t", [96, 1], F32)
        P.dma("sp", invf_t[:], invf, writes=["invf"])
        for r in range(S // RC):
            sl = slice(r * RC, (r + 1) * RC)
            P.dma("sp", pos_i[:], pos_in[:, sl], writes=["pos_i"])
            P.op("dve", lambda: dve.tensor_copy(out=ang[:], in_=pos_i[:]), reads=["pos_i"], writes=["ang"])
            P.op("dve", lambda: dve.tensor_scalar(out=ang[:], in0=ang[:], scalar1=invf_t[:, 0:1], scalar2=None,
                                                  op0=ALU.mult), reads=["ang", "invf"], writes=["ang"])
            for ti, (off, dst) in enumerate(((0.25, cos_s), (0.0, sin_s))):
                P.op("dve", lambda off=off: dve.tensor_scalar(
                    out=tmp[:], in0=ang[:], scalar1=1.0 / (2.0 * math.pi), scalar2=off, op0=ALU.mult, op1=ALU.add),
                    reads=["ang"], writes=["tmp"])
                P.op("dve", lambda: dve.tensor_copy(out=tmpi[:], in_=tmp[:]), reads=["tmp"], writes=["tmpi"])
                P.op("dve", lambda: dve.tensor_copy(out=tmp2[:], in_=tmpi[:]), reads=["tmpi"], writes=["tmp2"])
                P.op("dve", lambda: dve.tensor_tensor(out=tmp[:], in0=tmp[:], in1=tmp2[:], op=ALU.subtract),
                     reads=["tmp", "tmp2"], writes=["tmp"])
                P.op("dve", lambda: dve.tensor_scalar(out=tmp2[:], in0=tmp[:], scalar1=0.5, scalar2=None,
                                                      op0=ALU.is_ge), reads=["tmp"], writes=["tmp2"])
                P.op("dve", lambda: dve.tensor_tensor(out=tmp[:], in0=tmp[:], in1=tmp2[:], op=ALU.subtract),
                     reads=["tmp", "tmp2"], writes=["tmp"])
                P.op("act", lambda ti=ti: act.activation(out=tab[ti][:], in_=tmp[:], func=AF.Sin,
                                                         scale=2.0 * math.pi),
                     reads=["tmp"], writes=[("tab", ti)])
                P.dma("sp", dst[:, sl], tab[ti][:], reads=[("tab", ti)], writes=[("tabd", ti, r)])
    P.barrier()

    def norm_stats(xt, xkey, sqb, sd, rstd, nch=8, scale=1.0 / D, psb=0, tag="n"):
        for c in range(nch):
            P.op("act", lambda c=c: act.activation(out=sqb[c % 2][:], in_=xt[:, c, :], func=AF.Square),
                 reads=[xkey], writes=[("sqb", tag, c % 2)])
            P.op("pe", lambda c=c: pe.matmul(PS[psb][:], ONES, sqb[c % 2][:], start=(c == 0), stop=(c == nch - 1)),
                 reads=[("sqb", tag, c % 2), "cm"], writes=[pk(psb)])
        P.op("act", lambda: act.activation(out=sd[:], in_=PS[psb][:], func=AF.Ln, bias=eps_t[:, 0:1], scale=scale),
             reads=[pk(psb), "eps"], writes=[("sd", tag)])
        P.op("act", lambda: act.activation(out=rstd[:], in_=sd[:], func=AF.Exp, scale=-0.5),
             reads=[("sd", tag)], writes=[("rstd", tag)])

    def norm_apply(xt, xkey, hbuf, rstd, tb, sub, nch=8, tag="n", hkey=None):
        for c in range(nch):
            P.op("dve", lambda c=c: dve.tensor_tensor(out=tb[c % 2][:], in0=xt[:, c, :], in1=rstd[:], op=ALU.mult),
                 reads=[xkey, ("rstd", tag)], writes=[("tb", tag, c % 2)])
            P.op("act", lambda c=c: act.activation(
                out=hbuf[:, c, :], in_=tb[c % 2][:], func=AF.Identity,
                bias=modv[:, sub * 24 + c:sub * 24 + c + 1], scale=Asc[:, sub * 8 + c:sub * 8 + c + 1]),
                reads=[("tb", tag, c % 2), "modv", ("Asc", sub)], writes=[hkey if hkey is not None else ("h", tag)])

    def norm_front(xt, xkey, hbuf, sqb, sd, rstd, tb, sub, tag="n", lnexp=True):
        norm_stats(xt, xkey, sqb, sd, rstd, tag=tag)
        norm_apply(xt, xkey, hbuf, rstd, tb, sub, tag=tag)

    def ffn_phase(src, dst, l, sub, pname, pre=None):
        with ExitStack() as ph:
            if pre is None:
                w13t, w2t = ffn_weights(ph, l, pname)
            elif pre[1] is None:
                w13t = pre[0]
                w2t = ffn_w2(ph, l, pname)
                for kc in range(8):
                    pass
            else:
                w13t, w2t = pre
            xt = sb(ph, pname + "x", [128, 8, TT], F32)
            hb = sb(ph, pname + "h", [128, 8, TT], BF16)
            actb = sb(ph, pname + "act", [128, NF, TT], BF16)
            sqb = [sb(ph, pname + "sq%d" % i, [128, TT], BF16) for i in range(2)]
            tb = [sb(ph, pname + "tb%d" % i, [128, TT], F32) for i in range(2)]
            sg = [sb(ph, pname + "sg%d" % i, [128, TT], F32) for i in range(2)]
            sd = sb(ph, pname + "sd", [128, TT], F32)
            rstd = sb(ph, pname + "rstd", [128, TT], F32)
            xr = [sb(ph, pname + "xr%d" % i, [128, TT], F32) for i in range(3)]
            srcv = src.rearrange("(c p) s -> p c s", p=128)
            dstv = dst.rearrange("(c p) s -> p c s", p=128)

            def load_x(i):
                P.dma("sp", xt[:], srcv[:, :, i * TT:(i + 1) * TT], writes=["x"])

            def front(i):
                norm_front(xt, "x", hb, sqb, sd, rstd, tb, sub, tag=pname)

            def front_a(i):
                norm_stats(xt, "x", sqb, sd, rstd, tag=pname)

            def front_b(i):
                norm_apply(xt, "x", hb, rstd, tb, sub, tag=pname)

            load_x(0)
            front(0)
            xcnt = 0
            for i in range(NT):
                if i + 1 < NT:
                    load_x(i + 1)
                for j in range(NF):
                    gb, ub = 1 + (j % 2), 3 + (j % 2)
                    for kc in range(8):
                        P.op("pe", lambda j=j, kc=kc, gb=gb: pe.matmul(
                            PS[gb][:], w13t[:, kc, j * 128:(j + 1) * 128], hb[:, kc, :],
                            start=(kc == 0), stop=(kc == 7)),
                            reads=[("w13", kc), ("h", pname)], writes=[pk(gb)])
                    for kc in range(8):
                        P.op("pe", lambda j=j, kc=kc, ub=ub: pe.matmul(
                            PS[ub][:], w13t[:, kc, DFF + j * 128:DFF + (j + 1) * 128], hb[:, kc, :],
                            start=(kc == 0), stop=(kc == 7)),
                            reads=[("w13", kc), ("h", pname)], writes=[pk(ub)])
                    P.op("act", lambda j=j, gb=gb: act.activation(out=sg[j % 2][:], in_=PS[gb][:], func=AF.Silu),
                         reads=[pk(gb)], writes=[("sg", j % 2)])
                    P.op("dve", lambda j=j, ub=ub: dve.tensor_tensor(
                        out=actb[:, j, :], in0=sg[j % 2][:], in1=PS[ub][:], op=ALU.mult),
                        reads=[("sg", j % 2), pk(ub)], writes=[("act", j)])
                    if j == 3 and i + 1 < NT:
                        front_a(i + 1)
                if i + 1 < NT:
                    front_b(i + 1)
                for m in range(8):
                    yb = 5 + (m % 2)
                    xs = xcnt % 3
                    xcnt += 1
                    P.dma("sp", xr[xs][:], srcv[:, m, i * TT:(i + 1) * TT], writes=[("xr", xs)])
                    for j in range(NF):
                        P.op("pe", lambda j=j, m=m, yb=yb: pe.matmul(
                            PS[yb][:], w2t[:, j, m * 128:(m + 1) * 128], actb[:, j, :],
                            start=(j == 0), stop=(j == NF - 1)),
                            reads=[("w2", j), ("act", j)], writes=[pk(yb)])
                    P.op("dve", lambda m=m, yb=yb, xs=xs: dve.scalar_tensor_tensor(
                        out=xr[xs][:], in0=PS[yb][:], scalar=Gt[:, sub * 8 + m:sub * 8 + m + 1],
                        in1=xr[xs][:], op0=ALU.mult, op1=ALU.add),
                        reads=[pk(yb), ("xr", xs), ("Gt", sub)], writes=[("xr", xs)])
                    P.dma("sp", dstv[:, m, i * TT:(i + 1) * TT], xr[xs][:], reads=[("xr", xs)],
                          writes=[("dst", pname, i, m)])
        P.barrier()

    bank_rr = [0]

    def nb():
        b = 1 + bank_rr[0] % 7
        bank_rr[0] += 1
        return b

    def b1_phase():
        with ExitStack() as ph:
            wint = sb(ph, "wint", [128, 8, 4000], BF16)
            wuqt = sb(ph, "wuqt", [128, 2, 768], BF16)
            wukvt = sb(ph, "wukvt", [128, 1024], BF16)
            wb1t = sb(ph, "wb1t", [128, 4, D], BF16)
            xt = sb(ph, "b1x0", [128, 8, TT], F32)
            hbs = [sb(ph, "b1h%d" % i, [128, 8, TT], BF16) for i in range(2)]
            sqb = [sb(ph, "b1sq%d" % i, [128, TT], BF16) for i in range(2)]
            tb = [sb(ph, "b1tb%d" % i, [128, TT], F32) for i in range(2)]
            sd = sb(ph, "b1sd", [128, TT], F32)
            rstd = sb(ph, "b1rstd", [128, TT], F32)
            cqf = sb(ph, "cqf", [128, 2, TT], F32)
            sqq = sb(ph, "sqq", [128, 2, TT], BF16)
            cqn = [sb(ph, "cqn%d" % i, [128, 2, TT], BF16) for i in range(2)]
            ckvf = sb(ph, "ckvf", [128, TT], F32)
            sqk = sb(ph, "sqk", [128, TT], BF16)
            ckvn = [sb(ph, "ckvn%d" % i, [128, TT], BF16) for i in range(2)]
            sd2 = sd
            r2 = rstd
            krwf = sb(ph, "krwf", [32, TT], F32)
            krwb = sb(ph, "krwb", [32, TT], BF16)
            sqr = sb(ph, "sqr", [32, TT], BF16)
            ckt = sb(ph, "ckt", [32, TT], F32)
            skt = sb(ph, "skt", [32, TT], F32)
            krt2 = sb(ph, "krt2", [32, TT], F32)
            krf = sb(ph, "krf", [32, TT], BF16)
            xvs = [sb(ph, "xvs%d" % i, [128, TT], F32) for i in range(2)]
            ub = sb(ph, "ub", [128, 4, TT + 2], BF16)
            diagw = sb(ph, "diagw", [128, 12, 128], BF16)
            yv = [sb(ph, "yv%d" % i, [128, TT], F32) for i in range(2)]
            oc = sb(ph, "oc", [128, 4, TT], BF16)
            g0c = [sb(ph, "g0c%d" % i, [128, TT], BF16) for i in range(2)]
            g1b = sb(ph, "g1b", [128, 8, TT], BF16)
            cst = [sb(ph, "cst%d" % i, [96, TT], F32) for i in range(2)]
            snt = [sb(ph, "snt%d" % i, [96, TT], F32) for i in range(2)]
            qwf = [sb(ph, "qwf%d" % i, [128, TT], F32) for i in range(2)]
            qwb = [sb(ph, "qwb%d" % i, [96, TT], BF16) for i in range(2)]
            sq2 = [sb(ph, "sq2%d" % i, [128, TT], BF16) for i in range(2)]
            sdq = sb(ph, "sdq", [128, TT], F32)
            rq = sb(ph, "rq", [128, TT], F32)
            qt2 = [sb(ph, "qt2%d" % i, [96, TT], F32) for i in range(2)]
            qfb = [sb(ph, "qfb%d" % i, [128, TT], BF16) for i in range(2)]
            vb = [sb(ph, "vb%d" % i, [128, H, 65], BF16) for i in range(2)]

            w_in_v = w_in.rearrange("(c p) n -> p c n", p=128)
            for kc in range(8):
                P.dma("pool", wint[:, kc, :], w_in_v[:, kc, :], writes=[("win", kc)])
            P.dma("pool", wuqt[:], w_uq.rearrange("(c p) n -> p c n", p=128), writes=["wuq"])
            P.dma("pool", wukvt[:], w_ukv, writes=["wukv"])
            P.dma("pool", wb1t[:], w_br[1].rearrange("(c p) n -> p c n", p=128), writes=["wb1"])
            for c in range(4):
                P.op("dve", lambda c=c: dve.memset(ub[:, c, :], 0.0), writes=[("ub", c)])
            for j in range(12):
                P.op("dve", lambda j=j: dve.tensor_scalar(out=diagw[:, j, :], in0=IDENT, scalar1=cw_t[:, j:j + 1],
                                                          scalar2=None, op0=ALU.mult),
                     reads=["cm", "cw"], writes=["diagw"], cost=250.0)
            for i in range(2):
                P.op("dve", lambda i=i: dve.memset(vb[i][:], 1.0), writes=[("vb", i)])
            srcv = x1s.rearrange("(c p) s -> p c s", p=128)
            g0v = g0_s.rearrange("(c p) s -> p c s", p=128)
            ycv = yc_s.rearrange("(c p) s -> p c s", p=128)

            def proj(col0, M, b, tv):
                for kc in range(8):
                    P.op("pe", lambda kc=kc: pe.matmul(PS[b][0:M, :], wint[:, kc, col0:col0 + M], hbs[tv][:, kc, :],
                                                       start=(kc == 0), stop=(kc == 7)),
                         reads=[("win", kc), ("h", "b1", tv)], writes=[pk(b)])

            def N_step(i):
                tsl = slice(i * TT, (i + 1) * TT)
                tv = i % 2
                P.dma("sp", xt[:], srcv[:, :, tsl], writes=[("b1x", 0)])
                norm_stats(xt, ("b1x", 0), sqb, sd, rstd, tag="b1")
                norm_apply(xt, ("b1x", 0), hbs[tv], rstd, tb, 1, tag="b1", hkey=("h", "b1", tv))

            def X_steps(i):
                tsl = slice(i * TT, (i + 1) * TT)
                tv = i % 2
                steps = []

                def loads():
                    P.dma("sp", cst[tv][:], cos_s[:, tsl], writes=[("cst", tv)])
                    P.dma("sp", snt[tv][:], sin_s[:, tsl], writes=[("snt", tv)])
                    P.dma("sp", ckt[:], cos_s[64:96, tsl], writes=["ckt"])
                    P.dma("sp", skt[:], sin_s[64:96, tsl], writes=["skt"])
                steps.append(loads)
                banks = {}

                def cq_proj(c):
                    b = banks[("cq", c)] = nb()
                    proj(c * 128, 128, b, tv)
                    P.op("dve", lambda: dve.tensor_copy(out=cqf[:, c, :], in_=PS[b][:]),
                         reads=[pk(b)], writes=[("cqf", c)])
                    P.op("act", lambda: act.activation(out=sqq[:, c, :], in_=PS[b][:], func=AF.Square),
                         reads=[pk(b)], writes=[("sqq", c)])

                def cq_fin():
                    bs = nb()
                    for c in range(2):
                        P.op("pe", lambda c=c: pe.matmul(PS[bs][:], ONES, sqq[:, c, :], start=(c == 0), stop=(c == 1)),
                             reads=[("sqq", c), "cm"], writes=[pk(bs)])
                    P.op("act", lambda: act.activation(out=sd2[:], in_=PS[bs][:], func=AF.Ln, bias=eps_t[:, 0:1],
                                                       scale=1.0 / 256), reads=[pk(bs), "eps"], writes=[("sd", "b1")])
                    P.op("act", lambda: act.activation(out=r2[:], in_=sd2[:], func=AF.Exp, scale=-0.5),
                         reads=[("sd", "b1")], writes=[("rstd", "b1")])
                    for c in range(2):
                        P.op("dve", lambda c=c: dve.scalar_tensor_tensor(
                            out=cqn[tv][:, c, :], in0=cqf[:, c, :], scalar=qa_t[:, c:c + 1], in1=r2[:],
                            op0=ALU.mult, op1=ALU.mult), reads=[("cqf", c), ("rstd", "b1"), "qa"], writes=[("cqn", tv, c)])

                def ckv_proj():
                    b = nb()
                    proj(256, 128, b, tv)
                    P.op("dve", lambda: dve.tensor_copy(out=ckvf[:], in_=PS[b][:]),
                         reads=[pk(b)], writes=["ckvf"])
                    P.op("act", lambda: act.activation(out=sqk[:], in_=PS[b][:], func=AF.Square),
                         reads=[pk(b)], writes=["sqk"])

                def ckv_fin():
                    bs = nb()
                    P.op("pe", lambda: pe.matmul(PS[bs][:], ONES, sqk[:], start=True, stop=True),
                         reads=["sqk", "cm"], writes=[pk(bs)])
                    P.op("act", lambda: act.activation(out=sd2[:], in_=PS[bs][:], func=AF.Ln, bias=eps_t[:, 0:1],
                                                       scale=1.0 / 128), reads=[pk(bs), "eps"], writes=[("sd", "b1")])
                    P.op("act", lambda: act.activation(out=r2[:], in_=sd2[:], func=AF.Exp, scale=-0.5),
                         reads=[("sd", "b1")], writes=[("rstd", "b1")])
                    P.op("dve", lambda: dve.scalar_tensor_tensor(
                        out=ckvn[tv][:], in0=ckvf[:], scalar=kva_t[:, 0:1], in1=r2[:], op0=ALU.mult, op1=ALU.mult),
                        reads=["ckvf", ("rstd", "b1"), "kva"], writes=[("ckvn", tv)])

                def kr_proj():
                    b = nb()
                    proj(384, 32, b, tv)
                    P.op("act", lambda: act.activation(out=krwf[:], in_=PS[b][0:32, :], func=AF.Copy,
                                                       scale=krg_t[:, 0:1]), reads=[pk(b), "krg"], writes=["krwf"])
                    P.op("act", lambda: act.activation(out=krwb[:], in_=PS[b][0:32, :], func=AF.Copy,
                                                       scale=krg_t[:, 0:1]), reads=[pk(b), "krg"], writes=["krwb"])
                    P.op("act", lambda: act.activation(out=sqr[:], in_=PS[b][0:32, :], func=AF.Square),
                         reads=[pk(b)], writes=["sqr"])

                def kr_fin():
                    b2 = nb()
                    b3 = nb()
                    P.op("pe", lambda: pe.matmul(PS[b2][0:32, :], BO32, sqr[:], start=True, stop=True),
                         reads=["sqr", "cm"], writes=[pk(b2)])
                    P.op("pe", lambda: pe.matmul(PS[b3][0:32, :], R32T, krwb[:], start=True, stop=True),
                         reads=["krwb", "cm"], writes=[pk(b3)])
                    P.op("act", lambda: act.activation(out=sd2[0:32, :], in_=PS[b2][0:32, :], func=AF.Ln,
                                                       bias=eps_t[0:32, 0:1], scale=1.0),
                         reads=[pk(b2), "eps"], writes=[("sd", "b1")])
                    P.op("act", lambda: act.activation(out=r2[0:32, :], in_=sd2[0:32, :], func=AF.Exp, scale=-0.5),
                         reads=[("sd", "b1")], writes=[("rstd", "b1")])
                    P.op("dve", lambda: dve.tensor_tensor(out=krwf[:], in0=krwf[:], in1=ckt[:], op=ALU.mult),
                         reads=["krwf", "ckt"], writes=["krwf"])
                    P.op("dve", lambda: dve.tensor_tensor(out=krt2[:], in0=skt[:], in1=PS[b3][0:32, :], op=ALU.mult),
                         reads=[pk(b3), "skt"], writes=["krt2"])
                    P.op("dve", lambda: dve.tensor_tensor(out=krwf[:], in0=krwf[:], in1=krt2[:], op=ALU.add),
                         reads=["krwf", "krt2"], writes=["krwf"])
                    P.op("dve", lambda: dve.tensor_tensor(out=krf[:], in0=krwf[:], in1=r2[0:32, :], op=ALU.mult),
                         reads=["krwf", ("rstd", "b1")], writes=["krf"])
                    P.dma("sp", kr_s[:, tsl], krf[:], reads=["krf"], writes=[("krd", i)])

                def conv(c):
                    ba, bb, bc = nb(), nb(), nb()
                    proj(416 + c * 128, 128, ba, tv)
                    proj(416 + 1024 + c * 128, 128, bb, tv)
                    proj(416 + 512 + c * 128, 128, bc, tv)
                    v = c % 2
                    P.op("dve", lambda: dve.tensor_copy(out=xvs[v][:], in_=PS[ba][:]),
                         reads=[pk(ba)], writes=[("xvs", v)])
                    P.op("dve", lambda: dve.tensor_tensor(
                        out=ub[:, c, 2:2 + TT], in0=xvs[v][:], in1=PS[bb][:], op=ALU.mult),
                        reads=[("xvs", v), pk(bb)], writes=[("ub", c)])
                    P.op("dve", lambda: dve.tensor_copy(out=yv[v][:], in_=PS[bc][:]),
                         reads=[pk(bc)], writes=[("yv", v)])
                    by = nb()
                    for k in range(3):
                        P.op("pe", lambda k=k: pe.matmul(PS[by][:], diagw[:, c * 3 + k, :], ub[:, c, k:k + TT],
                                                         start=(k == 0), stop=(k == 2)),
                             reads=[("ub", c), "diagw"], writes=[pk(by)])
                    P.op("dve", lambda: dve.tensor_tensor(
                        out=oc[:, c, :], in0=yv[v][:], in1=PS[by][:], op=ALU.mult),
                        reads=[("yv", v), pk(by)], writes=[("oc", c)])
                    P.op("act", lambda: act.activation(out=ub[:, c, 0:2], in_=ub[:, c, TT:TT + 2], func=AF.Copy),
                         reads=[("ub", c)], writes=[("ub", c)], cost=200.0)

                def gate(gi):
                    b = nb()
                    proj(1952 + gi * 128, 128, b, tv)
                    if gi < 8:
                        v = gi % 2
                        if gi % 2 == 0:
                            P.op("act", lambda: act.activation(out=g0c[v][:], in_=PS[b][:], func=AF.Copy),
                                 reads=[pk(b)], writes=[("g0c", v)])
                        else:
                            P.op("dve", lambda: dve.tensor_copy(out=g0c[v][:], in_=PS[b][:]),
                                 reads=[pk(b)], writes=[("g0c", v)])
                        P.dma("sp", g0v[:, gi, tsl], g0c[v][:], reads=[("g0c", v)], writes=[("g0d", i, gi)])
                    else:
                        if gi % 2 == 0:
                            P.op("act", lambda: act.activation(out=g1b[:, gi - 8, :], in_=PS[b][:], func=AF.Copy),
                                 reads=[pk(b)], writes=[("g1", gi - 8)])
                        else:
                            P.op("dve", lambda: dve.tensor_copy(out=g1b[:, gi - 8, :], in_=PS[b][:]),
                                 reads=[pk(b)], writes=[("g1", gi - 8)])
                        if gi == 15:
                            P.op("act", lambda: act.activation(out=g1b[:], in_=g1b[:], func=AF.Sigmoid),
                                 reads=[("g1", mm) for mm in range(8)], writes=[("g1", mm) for mm in range(8)],
                                 cost=3700.0)

                def yc(m):
                    b = nb()
                    for c in range(4):
                        P.op("pe", lambda c=c: pe.matmul(
                            PS[b][:], wb1t[:, c, m * 128:(m + 1) * 128], oc[:, c, :], start=(c == 0), stop=(c == 3)),
                            reads=["wb1", ("oc", c)], writes=[pk(b)])
                    P.op("dve", lambda: dve.tensor_tensor(
                        out=g1b[:, m, :], in0=g1b[:, m, :], in1=PS[b][:], op=ALU.mult),
                        reads=[pk(b), ("g1", m)], writes=[("g1", m)])
                    if m == 7:
                        P.dma("sp", ycv[:, :, tsl], g1b[:], reads=[("g1", mm) for mm in range(8)],
                              writes=[("ycd", i)])

                steps.append(lambda: cq_proj(0))
                steps.append(lambda: cq_proj(1))
                steps.append(ckv_proj)
                steps.append(kr_proj)
                steps.append(lambda: conv(0))
                steps.append(cq_fin)
                steps.append(lambda: conv(1))
                steps.append(ckv_fin)
                steps.append(lambda: conv(2))
                steps.append(kr_fin)
                steps.append(lambda: conv(3))
                for gi in range(8, 16):
                    steps.append(lambda gi=gi: gate(gi))
                    if gi == 9 and i + 1 < NT:
                        steps.append(lambda: N_step(i + 1))
                for gi in range(8):
                    steps.append(lambda gi=gi: gate(gi))
                for m in range(8):
                    steps.append(lambda m=m: yc(m))
                return steps

            def Y_steps(i):
                tsl = slice(i * TT, (i + 1) * TT)
                tv = i % 2
                steps = []
                qb = {}

                def q1(h):
                    v = h % 2
                    b = qb[h] = nb()
                    for c in range(2):
                        P.op("pe", lambda c=c: pe.matmul(
                            PS[b][0:96, :], wuqt[:, c, h * 96:(h + 1) * 96], cqn[tv][:, c, :],
                            start=(c == 0), stop=(c == 1)),
                            reads=["wuq", ("cqn", tv, c)], writes=[pk(b)])
                    P.op("dve", lambda: dve.tensor_scalar(out=qwf[v][0:96, :], in0=PS[b][0:96, :],
                                                          scalar1=qg_t[:, 0:1], scalar2=None, op0=ALU.mult),
                         reads=[pk(b), "qg"], writes=[("qwf", v)])
                    P.op("act", lambda: act.activation(out=qwb[v][:], in_=PS[b][0:96, :], func=AF.Copy,
                                                       scale=qg_t[:, 0:1]),
                         reads=[pk(b), "qg"], writes=[("qwb", v)])
                    P.op("act", lambda: act.activation(out=sq2[v][0:96, :], in_=PS[b][0:96, :], func=AF.Square),
                         reads=[pk(b)], writes=[("sq2", v)])

                def q2(h):
                    v = h % 2
                    b2 = nb()
                    b3 = nb()
                    P.op("pe", lambda: pe.matmul(PS[b2][0:96, :], BO96, sq2[v][0:96, :], start=True, stop=True),
                         reads=[("sq2", v), "cm"], writes=[pk(b2)])
                    P.op("pe", lambda: pe.matmul(PS[b3][0:96, :], R96T, qwb[v][:], start=True, stop=True),
                         reads=[("qwb", v), "cm"], writes=[pk(b3)])
                    P.op("act", lambda: act.activation(out=sdq[0:96, :], in_=PS[b2][0:96, :], func=AF.Ln,
                                                       bias=eps_t[0:96, 0:1], scale=1.0),
                         reads=[pk(b2), "eps"], writes=["sdq"])
                    P.op("act", lambda: act.activation(out=rq[0:96, :], in_=sdq[0:96, :], func=AF.Exp, scale=-0.5),
                         reads=["sdq"], writes=["rq"])
                    P.op("dve", lambda: dve.tensor_tensor(out=qwf[v][0:96, :], in0=qwf[v][0:96, :], in1=cst[tv][:],
                                                          op=ALU.mult),
                         reads=[("qwf", v), ("cst", tv)], writes=[("qwf", v)])
                    P.op("dve", lambda: dve.tensor_tensor(out=qt2[v][:], in0=snt[tv][:], in1=PS[b3][0:96, :],
                                                          op=ALU.mult),
                         reads=[pk(b3), ("snt", tv)], writes=[("qt2", v)])
                    P.op("dve", lambda: dve.tensor_tensor(out=qwf[v][0:96, :], in0=qwf[v][0:96, :], in1=qt2[v][:],
                                                          op=ALU.add),
                         reads=[("qwf", v), ("qt2", v)], writes=[("qwf", v)])
                    P.op("dve", lambda: dve.tensor_tensor(out=qfb[v][0:96, :], in0=qwf[v][0:96, :], in1=rq[0:96, :],
                                                          op=ALU.mult),
                         reads=[("qwf", v), "rq"], writes=[("qfb", v)])
                    P.dma("sp", q_s[h][:, tsl], qfb[v][0:96, :], reads=[("qfb", v)], writes=[("qd", h, i)])

                def k1(hp):
                    v = hp % 2
                    b = qb[("k", hp)] = nb()
                    P.op("pe", lambda: pe.matmul(PS[b][:], wukvt[:, hp * 128:(hp + 1) * 128], ckvn[tv][:],
                                                 start=True, stop=True),
                         reads=["wukv", ("ckvn", tv)], writes=[pk(b)])
                    P.op("dve", lambda: dve.tensor_scalar(out=qwf[v][:], in0=PS[b][:], scalar1=kg_t[:, 0:1],
                                                          scalar2=None, op0=ALU.mult),
                         reads=[pk(b), "kg"], writes=[("qwf", v)])
                    P.op("act", lambda: act.activation(out=sq2[v][:], in_=PS[b][:], func=AF.Square),
                         reads=[pk(b)], writes=[("sq2", v)])

                def k2(hp):
                    v = hp % 2
                    b2 = nb()
                    P.op("pe", lambda: pe.matmul(PS[b2][:], BO128, sq2[v][:], start=True, stop=True),
                         reads=[("sq2", v), "cm"], writes=[pk(b2)])
                    P.op("act", lambda: act.activation(out=sdq[:], in_=PS[b2][:], func=AF.Ln,
                                                       bias=eps_t[:, 0:1], scale=1.0),
                         reads=[pk(b2), "eps"], writes=["sdq"])
                    P.op("act", lambda: act.activation(out=rq[:], in_=sdq[:], func=AF.Exp, scale=-0.5),
                         reads=["sdq"], writes=["rq"])
                    P.op("dve", lambda: dve.tensor_tensor(out=qfb[v][:], in0=qwf[v][:], in1=rq[:], op=ALU.mult),
                         reads=[("qwf", v), "rq"], writes=[("qfb", v)])
                    P.dma("sp", kn_s[hp][:, tsl], qfb[v][:], reads=[("qfb", v)], writes=[("knd", hp, i)])

                def vstep(sub):
                    v = sub % 2
                    b = nb()
                    P.op("pe", lambda: pe.matmul(PS[b][:], ckvn[tv][:, sub * 128:(sub + 1) * 128],
                                                 wukvt[:, 512:1024], start=True, stop=True),
                         reads=["wukv", ("ckvn", tv)], writes=[pk(b)])
                    P.op("act", lambda: act.activation(
                        out=vb[v][:, :, 0:64], in_=PS[b][:].rearrange("p (h d) -> p h d", h=H), func=AF.Copy),
                        reads=[pk(b)], writes=[("vb", v)])
                    P.dma("sp", v_s[i * 4 + sub], vb[v][:].rearrange("p h d -> p (h d)"), reads=[("vb", v)],
                          writes=[("vd", i, sub)])

                seq = [("q", h) for h in range(H)] + [("k", hp) for hp in range(4)]
                f1 = {"q": q1, "k": k1}
                f2 = {"q": q2, "k": k2}
                for n in range(len(seq) + 1):
                    if n < len(seq):
                        kind, a_ = seq[n]
                        steps.append(lambda kind=kind, a_=a_: f1[kind](a_))
                    if n >= 1:
                        kind, a_ = seq[n - 1]
                        steps.append(lambda kind=kind, a_=a_: f2[kind](a_))
                    if n in (2, 5, 8, 11):
                        sub = (2, 5, 8, 11).index(n)
                        steps.append(lambda sub=sub: vstep(sub))
                return steps

            def interleave(a, b_):
                out = []
                na, nb_ = len(a), len(b_)
                ia = ib = 0
                while ia < na or ib < nb_:
                    if ib >= nb_ or (ia < na and ia * nb_ <= ib * na):
                        out.append(a[ia]); ia += 1
                    else:
                        out.append(b_[ib]); ib += 1
                return out

            N_step(0)
            for st_ in X_steps(0):
                st_()
            for i in range(NT):
                ys = Y_steps(i)
                xs_ = X_steps(i + 1) if i + 1 < NT else []
                for st_ in interleave(ys, xs_):
                    st_()
        P.barrier()

    def b2_phase():
        sc = 96.0 ** -0.5
        with ExitStack() as ph:
            vt = sb(ph, "vt", [128, KT, H * 65], BF16)
            kts = [sb(ph, "kts%d" % i, [96, S], BF16) for i in range(2)]
            qts = [sb(ph, "qts%d" % i, [96, TT], BF16) for i in range(2)]
            pb = [sb(ph, "pb%d" % i, [128, TT], BF16) for i in range(4)]
            _osb = sb(ph, "osb0", [64, TT], F32); osb = [_osb, _osb]
            _rrow = sb(ph, "rrow0", [128, TT], F32); rrow = [_rrow, _rrow]
            _onb = sb(ph, "onb0", [64, TT], BF16); onb = [_onb, _onb]
            rhi = sb(ph, "rhi", [128, TT], BF16)
            rlo = sb(ph, "rlo", [128, TT], BF16)
            VG = min(8, KT)
            for k0 in range(0, KT, VG):
                P.dma("sp", vt[:, k0:k0 + VG, :], v_s[k0:k0 + VG].rearrange("k p f -> p k f"), writes=[("v", k0)])
            LOOK = 2
            events = []
            qcnt = 0
            ocnt = 0
            scnt = 0
            for h in range(H):
                ks = h % 2
                for i in range(NT):
                    qs = qcnt % 2
                    qcnt += 1
                    ob = 4 + (ocnt % 2)
                    ov = ocnt % 2
                    ocnt += 1
                    nk = 4 * i + 4
                    for kt in range(nk):
                        events.append(dict(h=h, i=i, kt=kt, nk=nk, ks=ks, qs=qs, ob=ob, ov=ov, sbk=scnt % 4))
                        scnt += 1

            def s_stage(e):
                h, i, kt, ks, qs, sbk = e["h"], e["i"], e["kt"], e["ks"], e["qs"], e["sbk"]
                if kt == 0:
                    if i == 0:
                        P.dma("sp", kts[ks][0:64, :], kn_s[h // 2][(h % 2) * 64:(h % 2) * 64 + 64, :],
                              writes=[("k", ks)])
                        P.dma("sp", kts[ks][64:96, :], kr_s[:, :], writes=[("k", ks)])
                    P.dma("sp", qts[qs][:], q_s[h][:, i * TT:(i + 1) * TT], writes=[("q", qs)])
                j = kt - 4 * i
                col0 = max(0, j) * 128
                e["col0"] = col0
                P.op("pe", lambda: pe.matmul(
                    PS[sbk][:, col0:TT], kts[ks][:, kt * 128:(kt + 1) * 128], qts[qs][:, col0:TT],
                    start=True, stop=True), reads=[("k", ks), ("q", qs)], writes=[pk(sbk)],
                    cost=(TT - col0) / 2.0 + 10.0)
                P.op("act", lambda: act.activation(
                    out=pb[sbk][:, col0:TT], in_=PS[sbk][:, col0:TT], func=AF.Exp, scale=sc),
                    reads=[pk(sbk)], writes=[("pb", sbk)], cost=(TT - col0 + 200) / 1.2)
                if j >= 0:
                    P.op("dve", lambda: dve.tensor_tensor(
                        out=pb[sbk][:, col0:col0 + 128], in0=pb[sbk][:, col0:col0 + 128], in1=TRI, op=ALU.mult),
                        reads=[("pb", sbk), "cm"], writes=[("pb", sbk)], cost=250.0)

            def pv_stage(e):
                h, kt, nk, sbk, ob, col0 = e["h"], e["kt"], e["nk"], e["sbk"], e["ob"], e["col0"]
                P.op("pe", lambda: pe.matmul(
                    PS[ob][0:65, col0:TT], vt[:, kt, h * 65:(h + 1) * 65], pb[sbk][:, col0:TT],
                    start=(kt == 0), stop=(kt == nk - 1)),
                    reads=[("v", (kt // VG) * VG), ("pb", sbk)], writes=[pk(ob)], cost=(TT - col0) / 2.0 + 10.0)

            def fin1(e):
                ob, ov = e["ob"], e["ov"]
                P.op("dve", lambda: dve.reciprocal(out=rrow[ov][64:65, :], in_=PS[ob][64:65, :]),
                     reads=[pk(ob)], writes=["rrow"], cost=3500.0)
                P.op("dve", lambda: dve.tensor_copy(out=osb[ov][:], in_=PS[ob][0:64, :]),
                     reads=[pk(ob)], writes=["osb"], cost=700.0)

            def fin2(e):
                h, i, ov = e["h"], e["i"], e["ov"]
                P.op("dve", lambda: dve.tensor_copy(out=rhi[64:65, :], in_=rrow[ov][64:65, :]),
                     reads=["rrow"], writes=["rhi"], cost=300.0)
                P.op("dve", lambda: dve.tensor_tensor(out=rlo[64:65, :], in0=rrow[ov][64:65, :], in1=rhi[64:65, :],
                                                      op=ALU.subtract),
                     reads=["rrow", "rhi"], writes=["rlo"], cost=300.0)
                P.op("pe", lambda: pe.matmul(PS[6][0:64, :], cm[64:65, 0:64], rhi[64:65, :], start=True, stop=False),
                     reads=["rhi", "cm"], writes=[pk(6)], cost=260.0)
                P.op("pe", lambda: pe.matmul(PS[6][0:64, :], cm[64:65, 0:64], rlo[64:65, :], start=False, stop=True),
                     reads=["rlo", "cm"], writes=[pk(6)], cost=260.0)
                P.op("dve", lambda: dve.tensor_tensor(out=onb[ov][:], in0=osb[ov][:], in1=PS[6][0:64, :],
                                                      op=ALU.mult),
                     reads=["osb", pk(6)], writes=["onb"])
                P.dma("sp", o_s[h][:, i * TT:(i + 1) * TT], onb[ov][:], reads=["onb"], writes=[("od", h, i)])

            deferred = []
            NE = len(events)
            for n in range(NE + LOOK + 3):
                if n < NE:
                    s_stage(events[n])
                m = n - LOOK
                if 0 <= m < NE:
                    e = events[m]
                    pv_stage(e)
                    if e["kt"] == e["nk"] - 1:
                        fin1(e)
                        deferred.append((n + 2, e))
                while deferred and deferred[0][0] <= n:
                    fin2(deferred.pop(0)[1])
            assert not deferred
        P.barrier()

    def c1_phase():
        with ExitStack() as ph:
            wb0t = sb(ph, "wb0t", [128, 4, D], BF16)
            woutt = sb(ph, "woutt", [128, 8, D], BF16)
            ots = [sb(ph, "ots%d" % i, [128, 4, TT], BF16) for i in range(2)]
            g0t = [sb(ph, "g0t%d" % i, [128, 8, TT], BF16) for i in range(2)]
            yct = [sb(ph, "yct%d" % i, [128, 8, TT], BF16) for i in range(2)]
            xts = [sb(ph, "c1x%d" % i, [128, 8, TT], F32) for i in range(2)]
            mg = sb(ph, "mg", [128, 8, TT], BF16)
            tmpf = [sb(ph, "c1t%d" % i, [128, TT], F32) for i in range(2)]
            P.dma("pool", wb0t[:], w_br[0].rearrange("(c p) n -> p c n", p=128), writes=["wb0"])
            P.dma("pool", woutt[:], w_out.rearrange("(c p) n -> p c n", p=128), writes=["wout"])
            x1v = x1s.rearrange("(c p) s -> p c s", p=128)
            x2v = x2s.rearrange("(c p) s -> p c s", p=128)
            g0v = g0_s.rearrange("(c p) s -> p c s", p=128)
            ycv = yc_s.rearrange("(c p) s -> p c s", p=128)
            osv = o_s.rearrange("(c two) d s -> two d c s", two=2)

            def loads(i):
                s_ = i % 2
                tsl = slice(i * TT, (i + 1) * TT)
                for two in range(2):
                    P.dma("sp", ots[s_][two * 64:(two + 1) * 64, :, :], osv[two][:, :, tsl], writes=[("ots", s_)])
                P.dma("sp", g0t[s_][:], g0v[:, :, tsl], writes=[("g0t", s_)])
                P.op("act", lambda s_=s_: act.activation(out=g0t[s_][:], in_=g0t[s_][:], func=AF.Sigmoid),
                     reads=[("g0t", s_)], writes=[("g0t", s_)], cost=3700.0)
                P.dma("sp", yct[s_][:], ycv[:, :, tsl], writes=[("yct", s_)])
                P.dma("sp", xts[s_][:], x1v[:, :, tsl], writes=[("c1x", s_)])

            loads(0)
            for i in range(NT):
                s_ = i % 2
                tsl = slice(i * TT, (i + 1) * TT)
                if i + 1 < NT:
                    loads(i + 1)
                for m in range(8):
                    b = nb()
                    for h in range(4):
                        P.op("pe", lambda h=h, m=m, b=b, s_=s_: pe.matmul(
                            PS[b][:], wb0t[:, h, m * 128:(m + 1) * 128], ots[s_][:, h, :],
                            start=(h == 0), stop=(h == 3)),
                            reads=["wb0", ("ots", s_)], writes=[pk(b)])
                    v = m % 2
                    P.op("dve", lambda m=m, b=b, s_=s_, v=v: dve.tensor_tensor(
                        out=tmpf[v][:], in0=g0t[s_][:, m, :], in1=PS[b][:], op=ALU.mult),
                        reads=[pk(b), ("g0t", s_)], writes=[("c1t", v)])
                    P.op("dve", lambda m=m, s_=s_, v=v: dve.tensor_tensor(
                        out=mg[:, m, :], in0=tmpf[v][:], in1=yct[s_][:, m, :], op=ALU.add),
                        reads=[("c1t", v), ("yct", s_)], writes=[("mg", m)])
                for m2 in range(8):
                    b = nb()
                    for m in range(8):
                        P.op("pe", lambda m=m, m2=m2, b=b: pe.matmul(
                            PS[b][:], woutt[:, m, m2 * 128:(m2 + 1) * 128], mg[:, m, :],
                            start=(m == 0), stop=(m == 7)),
                            reads=["wout", ("mg", m)], writes=[pk(b)])
                    P.op("dve", lambda m2=m2, b=b, s_=s_: dve.scalar_tensor_tensor(
                        out=xts[s_][:, m2, :], in0=PS[b][:], scalar=Gt[:, 8 + m2:8 + m2 + 1],
                        in1=xts[s_][:, m2, :], op0=ALU.mult, op1=ALU.add),
                        reads=[pk(b), ("c1x", s_), ("Gt", 1)], writes=[("c1x", s_)])
                P.dma("sp", x2v[:, :, tsl], xts[s_][:], reads=[("c1x", s_)], writes=[("x2d", i)])
        P.barrier()

    stop_after = getattr(build, "stop_after", None)
    if stop_after == "A":
        ffn_phase(xT, outT, 0, 0, "fa", pre=pre_ffn1)
        wst.close()
    else:
        ffn_phase(xT, x1s, 0, 0, "fa", pre=pre_ffn1)
        wst.close()
        b1_phase()
        if stop_after != "B1":
            wst2 = ExitStack()
            w13_2 = ffn_w13(wst2, 1, "fc")
            b2_phase()
            if stop_after != "B2":
                c1_phase()
                if stop_after != "C1":
                    ffn_phase(x2s, outT, 1, 2, "fc", pre=(w13_2, None))
            wst2.close()

    P.emit()
    build.est_ns = P.est_ns
    st.close()
    return nc


def host_inputs(inputs, S):
    f32 = np.float32
    x = np.asarray(inputs["x"], f32)
    B = x.shape[0]
    g = lambda k: np.asarray(inputs[k])
    inv = (1.0 / (10000.0 ** (np.arange(0, 32, 2, dtype=f32) / f32(32)))).astype(f32)
    invf = np.zeros((96, 1), f32)
    invf[64:80, 0] = inv
    invf[80:96, 0] = inv
    cm = np.zeros((128, 896), f32)
    cm[:, 768:896] = np.eye(128, dtype=f32)
    cm[:, 0:128] = 1.0
    cm[0:64, 128:128 + 64] = 1.0 / 64
    cm[64:96, 128 + 64:128 + 96] = 1.0 / 32
    cm[0:64, 256:256 + 64] = 1.0 / 64
    cm[64:128, 256 + 64:256 + 128] = 1.0 / 64
    for i in range(16):
        cm[80 + i, 384 + 64 + i] = -1.0
        cm[64 + i, 384 + 80 + i] = 1.0
        cm[16 + i, 512 + i] = -1.0
        cm[i, 512 + 16 + i] = 1.0
    cm[0:32, 512 + 32:512 + 64] = 1.0 / 32
    kk = np.arange(128)
    cm[:, 640:768] = (kk[None, :] >= kk[:, None]).astype(f32)
    wukv = g("w_ukv")[0].reshape(128, 8, 2, 64)
    wukv_p = np.concatenate([wukv[:, :, 0, :].reshape(128, 512), wukv[:, :, 1, :].reshape(128, 512)], axis=1)
    shared = {
        "invf": invf,
        "w_ada": np.ascontiguousarray(g("w_ada")[0], f32),
        "b_ada": np.ascontiguousarray(g("b_ada")[0].reshape(72, 128).T, f32),
        "normw": np.ascontiguousarray(g("norm_w")[0].reshape(3, 8, 128).transpose(2, 0, 1).reshape(128, 24), f32),
        "w13": np.ascontiguousarray(g("ffn_w13")[0], f32),
        "w2": np.ascontiguousarray(g("ffn_w2")[0], f32),
        "w_in": np.ascontiguousarray(g("w_in")[0], f32),
        "qa": np.ascontiguousarray(g("q_a_norm")[0].reshape(2, 128).T, f32),
        "kva": np.ascontiguousarray(g("kv_a_norm")[0].reshape(128, 1), f32),
        "w_uq": np.ascontiguousarray(g("w_uq")[0], f32),
        "w_ukv": np.ascontiguousarray(wukv_p, f32),
        "qgain": np.concatenate([g("q_norm_nope")[0], g("q_norm_rope")[0]]).reshape(96, 1).astype(f32),
        "kgain": np.concatenate([g("k_norm_nope")[0], g("k_norm_nope")[0]]).reshape(128, 1).astype(f32),
        "krgain": g("k_norm_rope")[0].reshape(32, 1).astype(f32),
        "convw": np.ascontiguousarray(g("conv_w")[0].T.reshape(4, 128, 3).transpose(1, 0, 2).reshape(128, 12), f32),
        "w_br": np.ascontiguousarray(g("w_branch")[0], f32),
        "w_out": np.ascontiguousarray(g("w_out")[0], f32),
        "cmat": cm,
    }
    maps = []
    for b in range(B):
        m = dict(shared)
        m["xT"] = np.ascontiguousarray(x[b, :S].T)
        m["c_in"] = np.ascontiguousarray(np.asarray(inputs["c"], f32)[b].reshape(8, 128).T)
        m["pos_in"] = np.ascontiguousarray(
            np.broadcast_to(np.asarray(inputs["positions"], np.int32)[b, :S][None, :], (96, S)))
        maps.append(m)
    return maps


_NC_CACHE = {}


def kernel(**inputs):
    S = 8192
    B = 8
    if S not in _NC_CACHE:
        _NC_CACHE[S] = build(S)
    nc = _NC_CACHE[S]
    maps = host_inputs(inputs, S)
    res = run_bass_kernel_spmd(nc, maps, core_ids=list(range(B)))
    out = np.empty((B, S, D), np.float32)
    for b in range(B):
        out[b] = np.asarray(res.results[b]["outT"]).T
    return out
```

```python
import math
from contextlib import ExitStack

import numpy as np
import ml_dtypes
import concourse.bass as bass
import concourse.mybir as mybir
from concourse.bass_utils import run_bass_kernel_spmd

F32 = mybir.dt.float32
BF16 = mybir.dt.bfloat16
I32 = mybir.dt.int32
ALU = mybir.AluOpType
AF = mybir.ActivationFunctionType

D = 1024
DFF = 2816
NF = DFF // 128
H = 8
EPS = 1e-6
TT = 512
NDMA_SEMS = 20


class Op:
    __slots__ = ("eng", "fn", "deps", "signal", "dma", "sem", "val", "idx", "cost", "seg", "sched",
                 "start", "fin", "pos", "sync", "bar")


DEF_COST = {"pe": 225.0, "act": 600.0, "dve": 680.0, "pool": 1300.0, "sp": 60.0}
SEM_LAT = 150.0
DMA_BW = 170.0
DMA_LAT = 1900.0
WINDOW = 224


class Prog:
    def __init__(self, nc, st):
        self.nc = nc
        self.ops = []
        self.lw = {}
        self.rd = {}
        self.seg = 0
        self.seg_ops = [[]]
        self.engs = {"pe": nc.tensor, "act": nc.scalar, "dve": nc.vector,
                     "pool": nc.gpsimd, "sp": nc.sync}
        self.esem = {e: st.enter_context(nc.semaphore("es_" + e)) for e in self.engs}
        self.dsem = {e: [st.enter_context(nc.semaphore("ds_%s_%d" % (e, i)))
                         for i in range(NDMA_SEMS)] for e in ("sp", "pool", "act")}
        self.do_sched = True

    def _new(self, eng, fn, dma, cost):
        op = Op()
        op.eng = eng; op.fn = fn; op.dma = dma; op.signal = False
        op.sem = None; op.val = 0; op.idx = len(self.ops); op.cost = cost
        op.seg = self.seg; op.sched = False; op.start = 0.0; op.fin = 0.0; op.pos = 0
        op.sync = None; op.bar = False; op.deps = {}
        return op

    def _add(self, eng, fn, reads, writes, dma, cost):
        op = self._new(eng, fn, dma, cost)
        deps = op.deps
        for k in reads:
            w = self.lw.get(k)
            if w is not None and w is not op:
                deps[id(w)] = (w, True)
            if type(k) is tuple and k[0] == "ps":
                for d in self.rd.get(k, ()):
                    if d.eng != eng and id(d) not in deps:
                        deps[id(d)] = (d, True)
        for k in writes:
            w = self.lw.get(k)
            if w is not None and w is not op and id(w) not in deps:
                deps[id(w)] = (w, False)
            for d in self.rd.get(k, ()):
                if d is not op and id(d) not in deps:
                    deps[id(d)] = (d, False)
        for k in reads:
            r = self.rd.get(k)
            if r is None:
                r = self.rd[k] = []
            r.append(op)
        for k in writes:
            self.lw[k] = op
            self.rd[k] = []
        self.ops.append(op)
        self.seg_ops[-1].append(op)
        return op

    def op(self, eng, fn, reads=(), writes=(), cost=None):
        return self._add(eng, fn, reads, writes, False, DEF_COST[eng] if cost is None else cost)

    def dma(self, eng, out, in_, reads=(), writes=(), nbytes=None):
        E = self.engs[eng]
        if nbytes is None:
            try:
                n = 1
                for d_ in out.shape:
                    n *= int(d_)
                nbytes = n * (2 if out.dtype == BF16 else 4)
            except Exception:
                nbytes = 262144
        return self._add(eng, lambda: E.dma_start(out=out, in_=in_), reads, writes, True, float(nbytes))

    def barrier(self):
        seg = self.seg_ops[-1]
        prev = None
        bars = []
        for e in ("sp", "pool", "act", "dve", "pe"):
            E = self.engs[e]
            op = self._new(e, (lambda E=E: E.nop()), False, 50.0)
            op.bar = True
            if prev is None:
                op.deps = {id(d): (d, True) for d in seg}
            else:
                op.deps = {id(prev): (prev, True)}
            self.ops.append(op)
            bars.append(op)
            prev = op
        self.seg_ops.append(bars)
        self.seg += 1
        self.seg_ops.append([])
        self.seg += 1
        for op in bars:
            op.seg = self.seg - 1
        self.lw = {}
        self.rd = {}

    def _schedule_segment(self, ops, t0):
        if not ops:
            return [], t0
        if ops[0].bar or not self.do_sched:
            t = t0
            for op in ops:
                op.sched = True
                op.start = t
                t += 1.0
                op.fin = t
            return list(ops), t
        queues = {e: [] for e in self.engs}
        for op in ops:
            queues[op.eng].append(op)
        head = {e: 0 for e in self.engs}
        free = {e: t0 for e in self.engs}
        dma_free = t0
        left = len(ops)
        out = []
        tmax = t0
        while left:
            best = None
            for e, q in queues.items():
                h = head[e]
                nq = len(q)
                while h < nq and q[h].sched:
                    h += 1
                head[e] = h
                if h >= nq:
                    continue
                fe = free[e]
                cand = None
                cnt = 0
                i = h
                while i < nq and cnt < WINDOW:
                    op = q[i]
                    i += 1
                    if op.sched:
                        continue
                    cnt += 1
                    ok = True
                    rt = t0
                    for d, raw in op.deps.values():
                        if not d.sched:
                            ok = False
                            break
                        f = d.fin
                        if d.dma or d.eng != e:
                            f += SEM_LAT
                        if f > rt:
                            rt = f
                    if not ok:
                        continue
                    stt = rt if rt > fe else fe
                    if cand is None or stt < cand[0]:
                        cand = (stt, op)
                    if stt <= fe:
                        break
                if cand is not None and (best is None or cand[0] < best[0]):
                    best = (cand[0], cand[1], e)
            assert best is not None, "scheduler stuck"
            stt, op, e = best
            op.sched = True
            op.start = stt
            if op.dma:
                free[e] = stt + DEF_COST["sp"]
                tb_ = stt if stt > dma_free else dma_free
                dma_free = tb_ + op.cost / DMA_BW
                op.fin = dma_free + DMA_LAT
            else:
                op.fin = stt + op.cost
                free[e] = op.fin
            if op.fin > tmax:
                tmax = op.fin
            out.append(op)
            left -= 1
        out.sort(key=lambda o: (o.start, o.idx))
        return out, tmax

    def emit(self):
        order = []
        t = 0.0
        for seg in self.seg_ops:
            o, t = self._schedule_segment(seg, t)
            order.extend(o)
        self.est_ns = t
        pos = {e: 0 for e in self.engs}
        for op in order:
            op.pos = pos[op.eng]
            pos[op.eng] += 1
        for op in order:
            best = {}
            sync = []
            for d, raw in op.deps.values():
                if d.dma:
                    sync.append(d)
                elif op.dma or d.eng != op.eng or (raw and op.eng != "pe"):
                    b = best.get(d.eng)
                    if b is None or d.pos > b.pos:
                        best[d.eng] = d
            sync.extend(best.values())
            for d in sync:
                d.signal = True
            op.sync = sync
        cnt = {e: 0 for e in self.engs}
        waited = {e: {} for e in self.engs}
        rr = {e: 0 for e in self.engs}
        dcnt = {}
        for op in order:
            E = self.engs[op.eng]
            need = {}
            for d in op.sync:
                s = d.sem
                assert s is not None, "dep without sem"
                key = id(s)
                if key not in need or need[key][1] < d.val:
                    need[key] = (s, d.val)
            if op.dma and op.signal:
                sems = self.dsem[op.eng]
                s = sems[rr[op.eng] % len(sems)]
                rr[op.eng] += 1
                prev = dcnt.get(id(s), 0)
                if prev > 0:
                    key = id(s)
                    if key not in need or need[key][1] < prev:
                        need[key] = (s, prev)
                op.sem = s
                op.val = prev + 16
                dcnt[id(s)] = op.val
            w = waited[op.eng]
            for key, (s, v) in need.items():
                if w.get(key, 0) < v:
                    E.wait_ge(s, v)
                    w[key] = v
            ins = op.fn()
            if op.signal:
                if op.dma:
                    ins.then_inc(op.sem, 16)
                else:
                    cnt[op.eng] += 1
                    op.sem = self.esem[op.eng]
                    op.val = cnt[op.eng]
                    ins.then_inc(op.sem, 1)


class K:
    pass


def build(S, debug_outs=()):
    NT = S // TT
    KT = S // 128
    nc = bass.Bass("TRN2", target_bir_lowering=False)
    st = ExitStack()
    P = Prog(nc, st)
    pe, act, dve, pool = nc.tensor, nc.scalar, nc.vector, nc.gpsimd

    def din(name, shape, dt=F32):
        return nc.dram_tensor(name, list(shape), dt, kind="ExternalInput").ap()

    def dscr(name, shape, dt):
        kind = "ExternalOutput" if name in debug_outs else "Internal"
        return nc.dram_tensor(name, list(shape), dt, kind=kind).ap()

    xT = din("xT", [D, S])
    c_in = din("c_in", [128, 8])
    pos_in = din("pos_in", [96, S], I32)
    invf = din("invf", [96, 1])
    w_ada = din("w_ada", [D, 9 * D])
    b_ada = din("b_ada", [128, 72])
    normw = din("normw", [128, 24])
    w13 = din("w13", [2, D, 2 * DFF])
    w2 = din("w2", [2, DFF, D])
    w_in = din("w_in", [D, 4000])
    qa = din("qa", [128, 2])
    kva = din("kva", [128, 1])
    w_uq = din("w_uq", [256, 768])
    w_ukv = din("w_ukv", [128, 1024])
    qgain = din("qgain", [96, 1])
    kgain = din("kgain", [128, 1])
    krgain = din("krgain", [32, 1])
    convw = din("convw", [128, 12])
    w_br = din("w_br", [2, 512, D])
    w_out = din("w_out", [D, D])
    cmat = din("cmat", [128, 7 * 128])
    outT = nc.dram_tensor("outT", [D, S], F32, kind="ExternalOutput").ap()

    x1s = dscr("x1s", [D, S], F32)
    x2s = dscr("x2s", [D, S], F32)
    cos_s = dscr("cos_s", [96, S], F32)
    sin_s = dscr("sin_s", [96, S], F32)
    q_s = dscr("q_s", [H, 96, S], BF16)
    kn_s = dscr("kn_s", [4, 128, S], BF16)
    kr_s = dscr("kr_s", [32, S], BF16)
    v_s = dscr("v_s", [KT, 128, H * 65], BF16)
    o_s = dscr("o_s", [H, 64, S], BF16)
    g0_s = dscr("g0_s", [D, S], BF16)
    yc_s = dscr("yc_s", [D, S], BF16)

    def sb(stk, name, shape, dt):
        return stk.enter_context(nc.sbuf_tensor(name, list(shape), dt))

    PS = [st.enter_context(nc.psum_tensor("ps%d" % i, [128, 512], F32)) for i in range(8)]

    def pk(i):
        return ("ps", i)

    cm = sb(st, "cm", [128, 7 * 128], BF16)
    cmf = sb(st, "cmf", [128, 128], F32)
    modv = sb(st, "modv", [128, 72], F32)
    Asc = sb(st, "Asc", [128, 24], F32)
    Gt = sb(st, "Gt", [128, 24], F32)
    nw = sb(st, "nw", [128, 24], F32)
    qa_t = sb(st, "qa_t", [128, 2], F32)
    kva_t = sb(st, "kva_t", [128, 1], F32)
    qg_t = sb(st, "qg_t", [96, 1], F32)
    kg_t = sb(st, "kg_t", [128, 1], F32)
    krg_t = sb(st, "krg_t", [32, 1], F32)
    cw_t = sb(st, "cw_t", [128, 12], F32)
    eps_t = sb(st, "eps_t", [128, 1], F32)
    negpi_t = sb(st, "negpi_t", [128, 1], F32)

    ONES = cm[:, 0:128]
    BO96 = cm[0:96, 128:128 + 96]
    BO128 = cm[:, 256:384]
    R96T = cm[0:96, 384:384 + 96]
    R32T = cm[0:32, 512:512 + 32]
    BO32 = cm[0:32, 512 + 32:512 + 64]
    TRI = cm[:, 640:768]
    IDENT = cm[:, 768:896]

    P.dma("pool", cm[:], cmat, writes=["cm"])
    P.op("dve", lambda: dve.memset(cmf[:], 1.0), writes=["cmf"])
    P.op("dve", lambda: dve.memset(eps_t[:], EPS), writes=["eps"])
    P.op("dve", lambda: dve.memset(negpi_t[:], -math.pi), writes=["negpi"])
    for t, src, key in ((nw, normw, "nw"), (qa_t, qa, "qa"), (kva_t, kva, "kva"), (qg_t, qgain, "qg"),
                        (kg_t, kgain, "kg"), (krg_t, krgain, "krg"), (cw_t, convw, "cw")):
        P.dma("sp", t[:], src, writes=[key])

    def ffn_w13(stk, l, pname):
        w13t = sb(stk, pname + "w13", [128, 8, 2 * DFF], BF16)
        w13v = w13[l].rearrange("(c p) n -> p c n", p=128)
        for kc in range(8):
            for hf in range(2):
                P.dma("pool", w13t[:, kc, hf * DFF:(hf + 1) * DFF], w13v[:, kc, hf * DFF:(hf + 1) * DFF],
                      writes=[("w13", kc)])
        return w13t

    def ffn_w2(stk, l, pname):
        w2t = sb(stk, pname + "w2", [128, NF, D], BF16)
        w2v = w2[l].rearrange("(c p) n -> p c n", p=128)
        for j in range(NF):
            P.dma("pool", w2t[:, j, :], w2v[:, j, :], writes=[("w2", j)])
        return w2t

    def ffn_weights(stk, l, pname):
        return ffn_w13(stk, l, pname), ffn_w2(stk, l, pname)

    wst = ExitStack()
    pre_ffn1 = ffn_weights(wst, 0, "fa")

    with ExitStack() as ph:
        c_t = sb(ph, "c_t", [128, 8], F32)
        ca_t = sb(ph, "ca_t", [128, 8], F32)
        b_t = sb(ph, "b_t", [128, 72], F32)
        NB = 18
        CB = 9 * D // NB
        wa = [sb(ph, "wa%d" % i, [128, 8, CB], F32) for i in range(2)]
        P.dma("sp", c_t[:], c_in, writes=["c"])
        P.dma("sp", b_t[:], b_ada, writes=["b"])
        P.op("act", lambda: act.activation(out=ca_t[:], in_=c_t[:], func=AF.Silu),
             reads=["c"], writes=["ca"])
        w_ada_v = w_ada.rearrange("(c p) n -> p c n", p=128)
        for blk in range(NB):
            slot = blk % 2
            for kc in range(8):
                P.dma("sp", wa[slot][:, kc, :], w_ada_v[:, kc, blk * CB:(blk + 1) * CB],
                      writes=[("wa", slot, kc)])
            for jj in range(CB // 128):
                j = blk * (CB // 128) + jj
                for kc in range(8):
                    P.op("pe", lambda slot=slot, kc=kc, jj=jj, j=j: pe.matmul(
                        PS[7][:, j:j + 1], wa[slot][:, kc, jj * 128:(jj + 1) * 128],
                        ca_t[:, kc:kc + 1], start=(kc == 0), stop=(kc == 7)),
                        reads=[("wa", slot, kc), "ca"], writes=[pk(7)], cost=230.0)
        P.op("dve", lambda: dve.tensor_tensor(out=modv[:], in0=PS[7][:, 0:72], in1=b_t[:], op=ALU.add),
             reads=[pk(7), "b"], writes=["modv"])
        for s in range(3):
            P.op("dve", lambda s=s: dve.scalar_tensor_tensor(
                out=Asc[:, s * 8:(s + 1) * 8], in0=modv[:, s * 24 + 8:s * 24 + 16], scalar=1.0,
                in1=nw[:, s * 8:(s + 1) * 8], op0=ALU.add, op1=ALU.mult),
                reads=["modv", "nw"], writes=[("Asc", s)])
            P.op("dve", lambda s=s: dve.tensor_scalar(
                out=Gt[:, s * 8:(s + 1) * 8], in0=modv[:, s * 24 + 16:s * 24 + 24],
                scalar1=(1.0 if s == 1 else 0.5), scalar2=None, op0=ALU.mult),
                reads=["modv"], writes=[("Gt", s)])

        RC = min(1024, S)
        pos_i = sb(ph, "pos_i", [96, RC], I32)
        ang = sb(ph, "ang", [96, RC], F32)
        tmp = sb(ph, "tmp", [96, RC], F32)
        tmpi = sb(ph, "tmpi", [96, RC], I32)
        tmp2 = sb(ph, "tmp2", [96, RC], F32)
        tab = [sb(ph, "tab%d" % i, [96, RC], F32) for i in range(2)]
        invf_t = sb(ph, "invf_t", [96, 1], F32)
        P.dma("sp", invf_t[:], invf, writes=["invf"])
        for r in range(S // RC):
            sl = slice(r * RC, (r + 1) * RC)
            P.dma("sp", pos_i[:], pos_in[:, sl], writes=["pos_i"])
            P.op("dve", lambda: dve.tensor_copy(out=ang[:], in_=pos_i[:]), reads=["pos_i"], writes=["ang"])
            P.op("dve", lambda: dve.tensor_scalar(out=ang[:], in0=ang[:], scalar1=invf_t[:, 0:1], scalar2=None,
                                                  op0=ALU.mult), reads=["ang", "invf"], writes=["ang"])
            for ti, (off, dst) in enumerate(((0.25, cos_s), (0.0, sin_s))):
                P.op("dve", lambda off=off: dve.tensor_scalar(
                    out=tmp[:], in0=ang[:], scalar1=1.0 / (2.0 * math.pi), scalar2=off, op0=ALU.mult, op1=ALU.add),
                    reads=["ang"], writes=["tmp"])
                P.op("dve", lambda: dve.tensor_copy(out=tmpi[:], in_=tmp[:]), reads=["tmp"], writes=["tmpi"])
                P.op("dve", lambda: dve.tensor_copy(out=tmp2[:], in_=tmpi[:]), reads=["tmpi"], writes=["tmp2"])
                P.op("dve", lambda: dve.tensor_tensor(out=tmp[:], in0=tmp[:], in1=tmp2[:], op=ALU.subtract),
                     reads=["tmp", "tmp2"], writes=["tmp"])
                P.op("dve", lambda: dve.tensor_scalar(out=tmp2[:], in0=tmp[:], scalar1=0.5, scalar2=None,
                                                      op0=ALU.is_ge), reads=["tmp"], writes=["tmp2"])
                P.op("dve", lambda: dve.tensor_tensor(out=tmp[:], in0=tmp[:], in1=tmp2[:], op=ALU.subtract),
                     reads=["tmp", "tmp2"], writes=["tmp"])
                P.op("act", lambda ti=ti: act.activation(out=tab[ti][:], in_=tmp[:], func=AF.Sin,
                                                         scale=2.0 * math.pi),
                     reads=["tmp"], writes=[("tab", ti)])
                P.dma("sp", dst[:, sl], tab[ti][:], reads=[("tab", ti)], writes=[("tabd", ti, r)])
    P.barrier()

    def norm_stats(xt, xkey, sqb, sd, rstd, nch=8, scale=1.0 / D, psb=0, tag="n"):
        for c in range(nch):
            P.op("act", lambda c=c: act.activation(out=sqb[c % 2][:], in_=xt[:, c, :], func=AF.Square),
                 reads=[xkey], writes=[("sqb", tag, c % 2)])
            P.op("pe", lambda c=c: pe.matmul(PS[psb][:], ONES, sqb[c % 2][:], start=(c == 0), stop=(c == nch - 1)),
                 reads=[("sqb", tag, c % 2), "cm"], writes=[pk(psb)])
        P.op("act", lambda: act.activation(out=sd[:], in_=PS[psb][:], func=AF.Ln, bias=eps_t[:, 0:1], scale=scale),
             reads=[pk(psb), "eps"], writes=[("sd", tag)])
        P.op("act", lambda: act.activation(out=rstd[:], in_=sd[:], func=AF.Exp, scale=-0.5),
             reads=[("sd", tag)], writes=[("rstd", tag)])

    def norm_apply(xt, xkey, hbuf, rstd, tb, sub, nch=8, tag="n", hkey=None):
        for c in range(nch):
            P.op("dve", lambda c=c: dve.tensor_tensor(out=tb[c % 2][:], in0=xt[:, c, :], in1=rstd[:], op=ALU.mult),
                 reads=[xkey, ("rstd", tag)], writes=[("tb", tag, c % 2)])
            P.op("act", lambda c=c: act.activation(
                out=hbuf[:, c, :], in_=tb[c % 2][:], func=AF.Identity,
                bias=modv[:, sub * 24 + c:sub * 24 + c + 1], scale=Asc[:, sub * 8 + c:sub * 8 + c + 1]),
                reads=[("tb", tag, c % 2), "modv", ("Asc", sub)], writes=[hkey if hkey is not None else ("h", tag)],
                cost=790.0)

    def norm_front(xt, xkey, hbuf, sqb, sd, rstd, tb, sub, tag="n", lnexp=True):
        norm_stats(xt, xkey, sqb, sd, rstd, tag=tag)
        norm_apply(xt, xkey, hbuf, rstd, tb, sub, tag=tag)

    def ffn_phase(src, dst, l, sub, pname, pre=None):
        with ExitStack() as ph:
            if pre is None:
                w13t, w2t = ffn_weights(ph, l, pname)
            elif pre[1] is None:
                w13t = pre[0]
                w2t = ffn_w2(ph, l, pname)
                for kc in range(8):
                    pass
            else:
                w13t, w2t = pre
            xt = sb(ph, pname + "x", [128, 8, TT], F32)
            hb = sb(ph, pname + "h", [128, 8, TT], BF16)
            actb = sb(ph, pname + "act", [128, NF, TT], BF16)
            sqb = [sb(ph, pname + "sq%d" % i, [128, TT], BF16) for i in range(2)]
            tb = [sb(ph, pname + "tb%d" % i, [128, TT], F32) for i in range(2)]
            sg = [sb(ph, pname + "sg%d" % i, [128, TT], F32) for i in range(2)]
            sd = sb(ph, pname + "sd", [128, TT], F32)
            rstd = sb(ph, pname + "rstd", [128, TT], F32)
            xr = [sb(ph, pname + "xr%d" % i, [128, TT], F32) for i in range(3)]
            srcv = src.rearrange("(c p) s -> p c s", p=128)
            dstv = dst.rearrange("(c p) s -> p c s", p=128)

            def load_x(i):
                P.dma("sp", xt[:], srcv[:, :, i * TT:(i + 1) * TT], writes=["x"])

            def front(i):
                norm_front(xt, "x", hb, sqb, sd, rstd, tb, sub, tag=pname)

            def front_a(i):
                norm_stats(xt, "x", sqb, sd, rstd, tag=pname)

            def front_b(i):
                norm_apply(xt, "x", hb, rstd, tb, sub, tag=pname)

            load_x(0)
            front(0)
            xcnt = 0
            for i in range(NT):
                if i + 1 < NT:
                    load_x(i + 1)
                for j in range(NF):
                    gb, ub = 1 + (j % 2), 3 + (j % 2)
                    for kc in range(8):
                        P.op("pe", lambda j=j, kc=kc, gb=gb: pe.matmul(
                            PS[gb][:], w13t[:, kc, j * 128:(j + 1) * 128], hb[:, kc, :],
                            start=(kc == 0), stop=(kc == 7)),
                            reads=[("w13", kc), ("h", pname)], writes=[pk(gb)])
                    for kc in range(8):
                        P.op("pe", lambda j=j, kc=kc, ub=ub: pe.matmul(
                            PS[ub][:], w13t[:, kc, DFF + j * 128:DFF + (j + 1) * 128], hb[:, kc, :],
                            start=(kc == 0), stop=(kc == 7)),
                            reads=[("w13", kc), ("h", pname)], writes=[pk(ub)])
                    P.op("act", lambda j=j, gb=gb: act.activation(out=sg[j % 2][:], in_=PS[gb][:], func=AF.Silu),
                         reads=[pk(gb)], writes=[("sg", j % 2)])
                    P.op("dve", lambda j=j, ub=ub: dve.tensor_tensor(
                        out=actb[:, j, :], in0=sg[j % 2][:], in1=PS[ub][:], op=ALU.mult),
                        reads=[("sg", j % 2), pk(ub)], writes=[("act", j)])
                    if j == 3 and i + 1 < NT:
                        front_a(i + 1)
                if i + 1 < NT:
                    front_b(i + 1)
                for m in range(8):
                    yb = 5 + (m % 2)
                    xs = xcnt % 3
                    xcnt += 1
                    P.dma("sp", xr[xs][:], srcv[:, m, i * TT:(i + 1) * TT], writes=[("xr", xs)])
                    for j in range(NF):
                        P.op("pe", lambda j=j, m=m, yb=yb: pe.matmul(
                            PS[yb][:], w2t[:, j, m * 128:(m + 1) * 128], actb[:, j, :],
                            start=(j == 0), stop=(j == NF - 1)),
                            reads=[("w2", j), ("act", j)], writes=[pk(yb)])
                    P.op("dve", lambda m=m, yb=yb, xs=xs: dve.scalar_tensor_tensor(
                        out=xr[xs][:], in0=PS[yb][:], scalar=Gt[:, sub * 8 + m:sub * 8 + m + 1],
                        in1=xr[xs][:], op0=ALU.mult, op1=ALU.add),
                        reads=[pk(yb), ("xr", xs), ("Gt", sub)], writes=[("xr", xs)])
                    P.dma("sp", dstv[:, m, i * TT:(i + 1) * TT], xr[xs][:], reads=[("xr", xs)],
                          writes=[("dst", pname, i, m)])
        P.barrier()

    bank_rr = [0]

    def nb():
        b = 1 + bank_rr[0] % 7
        bank_rr[0] += 1
        return b

    def b1_phase():
        with ExitStack() as ph:
            wint = sb(ph, "wint", [128, 8, 4000], BF16)
            wuqt = sb(ph, "wuqt", [128, 2, 768], BF16)
            wukvt = sb(ph, "wukvt", [128, 1024], BF16)
            wb1t = sb(ph, "wb1t", [128, 4, D], BF16)
            xt = sb(ph, "b1x0", [128, 8, TT], F32)
            hbs = [sb(ph, "b1h%d" % i, [128, 8, TT], BF16) for i in range(2)]
            sqb = [sb(ph, "b1sq%d" % i, [128, TT], BF16) for i in range(2)]
            tb = [sb(ph, "b1tb%d" % i, [128, TT], F32) for i in range(2)]
            sd = sb(ph, "b1sd", [128, TT], F32)
            rstd = sb(ph, "b1rstd", [128, TT], F32)
            cqf = sb(ph, "cqf", [128, 2, TT], F32)
            sqq = sb(ph, "sqq", [128, 2, TT], BF16)
            cqn = [sb(ph, "cqn%d" % i, [128, 2, TT], BF16) for i in range(2)]
            ckvf = sb(ph, "ckvf", [128, TT], F32)
            sqk = sb(ph, "sqk", [128, TT], BF16)
            ckvn = [sb(ph, "ckvn%d" % i, [128, TT], BF16) for i in range(2)]
            sd2 = sd
            r2 = rstd
            krwf = sb(ph, "krwf", [32, TT], F32)
            krwb = sb(ph, "krwb", [32, TT], BF16)
            sqr = sb(ph, "sqr", [32, TT], BF16)
            ckt = sb(ph, "ckt", [32, TT], F32)
            skt = sb(ph, "skt", [32, TT], F32)
            krt2 = sb(ph, "krt2", [32, TT], F32)
            krf = sb(ph, "krf", [32, TT], BF16)
            xvs = [sb(ph, "xvs%d" % i, [128, TT], F32) for i in range(2)]
            ub = sb(ph, "ub", [128, 4, TT + 2], BF16)
            diagw = sb(ph, "diagw", [128, 12, 128], BF16)
            yv = [sb(ph, "yv%d" % i, [128, TT], F32) for i in range(2)]
            oc = sb(ph, "oc", [128, 4, TT], BF16)
            g0c = [sb(ph, "g0c%d" % i, [128, TT], BF16) for i in range(2)]
            g1b = sb(ph, "g1b", [128, 8, TT], BF16)
            cst = [sb(ph, "cst%d" % i, [96, TT], F32) for i in range(2)]
            snt = [sb(ph, "snt%d" % i, [96, TT], F32) for i in range(2)]
            qwf = [sb(ph, "qwf%d" % i, [128, TT], F32) for i in range(2)]
            qwb = [sb(ph, "qwb%d" % i, [96, TT], BF16) for i in range(2)]
            sq2 = [sb(ph, "sq2%d" % i, [128, TT], BF16) for i in range(2)]
            sdq = sb(ph, "sdq", [128, TT], F32)
            rq = sb(ph, "rq", [128, TT], F32)
            qt2 = [sb(ph, "qt2%d" % i, [96, TT], F32) for i in range(2)]
            qfb = [sb(ph, "qfb%d" % i, [128, TT], BF16) for i in range(2)]
            vb = [sb(ph, "vb%d" % i, [128, H, 65], BF16) for i in range(2)]

            w_in_v = w_in.rearrange("(c p) n -> p c n", p=128)
            for kc in range(8):
                P.dma("pool", wint[:, kc, :], w_in_v[:, kc, :], writes=[("win", kc)])
            P.dma("pool", wuqt[:], w_uq.rearrange("(c p) n -> p c n", p=128), writes=["wuq"])
            P.dma("pool", wukvt[:], w_ukv, writes=["wukv"])
            P.dma("pool", wb1t[:], w_br[1].rearrange("(c p) n -> p c n", p=128), writes=["wb1"])
            for c in range(4):
                P.op("dve", lambda c=c: dve.memset(ub[:, c, :], 0.0), writes=[("ub", c)])
            for j in range(12):
                P.op("dve", lambda j=j: dve.tensor_scalar(out=diagw[:, j, :], in0=IDENT, scalar1=cw_t[:, j:j + 1],
                                                          scalar2=None, op0=ALU.mult),
                     reads=["cm", "cw"], writes=["diagw"], cost=250.0)
            for i in range(2):
                P.op("dve", lambda i=i: dve.memset(vb[i][:], 1.0), writes=[("vb", i)])
            srcv = x1s.rearrange("(c p) s -> p c s", p=128)
            g0v = g0_s.rearrange("(c p) s -> p c s", p=128)
            ycv = yc_s.rearrange("(c p) s -> p c s", p=128)

            def proj(col0, M, b, tv):
                for kc in range(8):
                    P.op("pe", lambda kc=kc: pe.matmul(PS[b][0:M, :], wint[:, kc, col0:col0 + M], hbs[tv][:, kc, :],
                                                       start=(kc == 0), stop=(kc == 7)),
                         reads=[("win", kc), ("h", "b1", tv)], writes=[pk(b)])

            def N_step(i):
                tsl = slice(i * TT, (i + 1) * TT)
                tv = i % 2
                P.dma("sp", xt[:], srcv[:, :, tsl], writes=[("b1x", 0)])
                norm_stats(xt, ("b1x", 0), sqb, sd, rstd, tag="b1")
                norm_apply(xt, ("b1x", 0), hbs[tv], rstd, tb, 1, tag="b1", hkey=("h", "b1", tv))

            def X_steps(i):
                tsl = slice(i * TT, (i + 1) * TT)
                tv = i % 2
                steps = []

                def loads():
                    P.dma("sp", cst[tv][:], cos_s[:, tsl], writes=[("cst", tv)])
                    P.dma("sp", snt[tv][:], sin_s[:, tsl], writes=[("snt", tv)])
                    P.dma("sp", ckt[:], cos_s[64:96, tsl], writes=["ckt"])
                    P.dma("sp", skt[:], sin_s[64:96, tsl], writes=["skt"])
                steps.append(loads)
                banks = {}

                def cq_proj(c):
                    b = banks[("cq", c)] = nb()
                    proj(c * 128, 128, b, tv)
                    P.op("dve", lambda: dve.tensor_copy(out=cqf[:, c, :], in_=PS[b][:]),
                         reads=[pk(b)], writes=[("cqf", c)])
                    P.op("act", lambda: act.activation(out=sqq[:, c, :], in_=PS[b][:], func=AF.Square),
                         reads=[pk(b)], writes=[("sqq", c)])

                def cq_fin():
                    bs = nb()
                    for c in range(2):
                        P.op("pe", lambda c=c: pe.matmul(PS[bs][:], ONES, sqq[:, c, :], start=(c == 0), stop=(c == 1)),
                             reads=[("sqq", c), "cm"], writes=[pk(bs)])
                    P.op("act", lambda: act.activation(out=sd2[:], in_=PS[bs][:], func=AF.Ln, bias=eps_t[:, 0:1],
                                                       scale=1.0 / 256), reads=[pk(bs), "eps"], writes=[("sd", "b1")])
                    P.op("act", lambda: act.activation(out=r2[:], in_=sd2[:], func=AF.Exp, scale=-0.5),
                         reads=[("sd", "b1")], writes=[("rstd", "b1")])
                    for c in range(2):
                        P.op("dve", lambda c=c: dve.scalar_tensor_tensor(
                            out=cqn[tv][:, c, :], in0=cqf[:, c, :], scalar=qa_t[:, c:c + 1], in1=r2[:],
                            op0=ALU.mult, op1=ALU.mult), reads=[("cqf", c), ("rstd", "b1"), "qa"], writes=[("cqn", tv, c)])

                def ckv_proj():
                    b = nb()
                    proj(256, 128, b, tv)
                    P.op("dve", lambda: dve.tensor_copy(out=ckvf[:], in_=PS[b][:]),
                         reads=[pk(b)], writes=["ckvf"])
                    P.op("act", lambda: act.activation(out=sqk[:], in_=PS[b][:], func=AF.Square),
                         reads=[pk(b)], writes=["sqk"])

                def ckv_fin():
                    bs = nb()
                    P.op("pe", lambda: pe.matmul(PS[bs][:], ONES, sqk[:], start=True, stop=True),
                         reads=["sqk", "cm"], writes=[pk(bs)])
                    P.op("act", lambda: act.activation(out=sd2[:], in_=PS[bs][:], func=AF.Ln, bias=eps_t[:, 0:1],
                                                       scale=1.0 / 128), reads=[pk(bs), "eps"], writes=[("sd", "b1")])
                    P.op("act", lambda: act.activation(out=r2[:], in_=sd2[:], func=AF.Exp, scale=-0.5),
                         reads=[("sd", "b1")], writes=[("rstd", "b1")])
                    P.op("dve", lambda: dve.scalar_tensor_tensor(
                        out=ckvn[tv][:], in0=ckvf[:], scalar=kva_t[:, 0:1], in1=r2[:], op0=ALU.mult, op1=ALU.mult),
                        reads=["ckvf", ("rstd", "b1"), "kva"], writes=[("ckvn", tv)])

                def kr_proj():
                    b = nb()
                    proj(384, 32, b, tv)
                    P.op("act", lambda: act.activation(out=krwf[:], in_=PS[b][0:32, :], func=AF.Copy,
                                                       scale=krg_t[:, 0:1]), reads=[pk(b), "krg"], writes=["krwf"])
                    P.op("act", lambda: act.activation(out=krwb[:], in_=PS[b][0:32, :], func=AF.Copy,
                                                       scale=krg_t[:, 0:1]), reads=[pk(b), "krg"], writes=["krwb"])
                    P.op("act", lambda: act.activation(out=sqr[:], in_=PS[b][0:32, :], func=AF.Square),
                         reads=[pk(b)], writes=["sqr"])

                def kr_fin():
                    b2 = nb()
                    b3 = nb()
                    P.op("pe", lambda: pe.matmul(PS[b2][0:32, :], BO32, sqr[:], start=True, stop=True),
                         reads=["sqr", "cm"], writes=[pk(b2)])
                    P.op("pe", lambda: pe.matmul(PS[b3][0:32, :], R32T, krwb[:], start=True, stop=True),
                         reads=["krwb", "cm"], writes=[pk(b3)])
                    P.op("act", lambda: act.activation(out=sd2[0:32, :], in_=PS[b2][0:32, :], func=AF.Ln,
                                                       bias=eps_t[0:32, 0:1], scale=1.0),
                         reads=[pk(b2), "eps"], writes=[("sd", "b1")])
                    P.op("act", lambda: act.activation(out=r2[0:32, :], in_=sd2[0:32, :], func=AF.Exp, scale=-0.5),
                         reads=[("sd", "b1")], writes=[("rstd", "b1")])
                    P.op("dve", lambda: dve.tensor_tensor(out=krwf[:], in0=krwf[:], in1=ckt[:], op=ALU.mult),
                         reads=["krwf", "ckt"], writes=["krwf"])
                    P.op("dve", lambda: dve.tensor_tensor(out=krt2[:], in0=skt[:], in1=PS[b3][0:32, :], op=ALU.mult),
                         reads=[pk(b3), "skt"], writes=["krt2"])
                    P.op("dve", lambda: dve.tensor_tensor(out=krwf[:], in0=krwf[:], in1=krt2[:], op=ALU.add),
                         reads=["krwf", "krt2"], writes=["krwf"])
                    P.op("dve", lambda: dve.tensor_tensor(out=krf[:], in0=krwf[:], in1=r2[0:32, :], op=ALU.mult),
                         reads=["krwf", ("rstd", "b1")], writes=["krf"])
                    P.dma("sp", kr_s[:, tsl], krf[:], reads=["krf"], writes=[("krd", i)])

                def conv(c):
                    ba, bb, bc = nb(), nb(), nb()
                    proj(416 + c * 128, 128, ba, tv)
                    proj(416 + 1024 + c * 128, 128, bb, tv)
                    proj(416 + 512 + c * 128, 128, bc, tv)
                    v = c % 2
                    P.op("dve", lambda: dve.tensor_copy(out=xvs[v][:], in_=PS[ba][:]),
                         reads=[pk(ba)], writes=[("xvs", v)])
                    P.op("dve", lambda: dve.tensor_tensor(
                        out=ub[:, c, 2:2 + TT], in0=xvs[v][:], in1=PS[bb][:], op=ALU.mult),
                        reads=[("xvs", v), pk(bb)], writes=[("ub", c)])
                    P.op("dve", lambda: dve.tensor_copy(out=yv[v][:], in_=PS[bc][:]),
                         reads=[pk(bc)], writes=[("yv", v)])
                    by = nb()
                    for k in range(3):
                        P.op("pe", lambda k=k: pe.matmul(PS[by][:], diagw[:, c * 3 + k, :], ub[:, c, k:k + TT],
                                                         start=(k == 0), stop=(k == 2)),
                             reads=[("ub", c), "diagw"], writes=[pk(by)])
                    P.op("dve", lambda: dve.tensor_tensor(
                        out=oc[:, c, :], in0=yv[v][:], in1=PS[by][:], op=ALU.mult),
                        reads=[("yv", v), pk(by)], writes=[("oc", c)])
                    P.op("act", lambda: act.activation(out=ub[:, c, 0:2], in_=ub[:, c, TT:TT + 2], func=AF.Copy),
                         reads=[("ub", c)], writes=[("ub", c)], cost=200.0)

                def gate(gi):
                    b = nb()
                    proj(1952 + gi * 128, 128, b, tv)
                    if gi < 8:
                        v = gi % 2
                        if gi % 2 == 0:
                            P.op("act", lambda: act.activation(out=g0c[v][:], in_=PS[b][:], func=AF.Copy),
                                 reads=[pk(b)], writes=[("g0c", v)])
                        else:
                            P.op("dve", lambda: dve.tensor_copy(out=g0c[v][:], in_=PS[b][:]),
                                 reads=[pk(b)], writes=[("g0c", v)])
                        P.dma("sp", g0v[:, gi, tsl], g0c[v][:], reads=[("g0c", v)], writes=[("g0d", i, gi)])
                    else:
                        if gi % 2 == 0:
                            P.op("act", lambda: act.activation(out=g1b[:, gi - 8, :], in_=PS[b][:], func=AF.Copy),
                                 reads=[pk(b)], writes=[("g1", gi - 8)])
                        else:
                            P.op("dve", lambda: dve.tensor_copy(out=g1b[:, gi - 8, :], in_=PS[b][:]),
                                 reads=[pk(b)], writes=[("g1", gi - 8)])
                        if gi == 15:
                            P.op("act", lambda: act.activation(out=g1b[:], in_=g1b[:], func=AF.Sigmoid),
                                 reads=[("g1", mm) for mm in range(8)], writes=[("g1", mm) for mm in range(8)],
                                 cost=3700.0)

                def yc(m):
                    b = nb()
                    for c in range(4):
                        P.op("pe", lambda c=c: pe.matmul(
                            PS[b][:], wb1t[:, c, m * 128:(m + 1) * 128], oc[:, c, :], start=(c == 0), stop=(c == 3)),
                            reads=["wb1", ("oc", c)], writes=[pk(b)])
                    P.op("dve", lambda: dve.tensor_tensor(
                        out=g1b[:, m, :], in0=g1b[:, m, :], in1=PS[b][:], op=ALU.mult),
                        reads=[pk(b), ("g1", m)], writes=[("g1", m)])
                    if m == 7:
                        P.dma("sp", ycv[:, :, tsl], g1b[:], reads=[("g1", mm) for mm in range(8)],
                              writes=[("ycd", i)])

                steps.append(lambda: cq_proj(0))
                steps.append(lambda: cq_proj(1))
                steps.append(ckv_proj)
                steps.append(kr_proj)
                steps.append(lambda: conv(0))
                steps.append(cq_fin)
                steps.append(lambda: conv(1))
                steps.append(ckv_fin)
                steps.append(lambda: conv(2))
                steps.append(kr_fin)
                steps.append(lambda: conv(3))
                for gi in range(8, 16):
                    steps.append(lambda gi=gi: gate(gi))
                    if gi == 9 and i + 1 < NT:
                        steps.append(lambda: N_step(i + 1))
                for gi in range(8):
                    steps.append(lambda gi=gi: gate(gi))
                for m in range(8):
                    steps.append(lambda m=m: yc(m))
                return steps

            def Y_steps(i):
                tsl = slice(i * TT, (i + 1) * TT)
                tv = i % 2
                steps = []
                qb = {}

                def q1(h):
                    v = h % 2
                    b = qb[h] = nb()
                    for c in range(2):
                        P.op("pe", lambda c=c: pe.matmul(
                            PS[b][0:96, :], wuqt[:, c, h * 96:(h + 1) * 96], cqn[tv][:, c, :],
                            start=(c == 0), stop=(c == 1)),
                            reads=["wuq", ("cqn", tv, c)], writes=[pk(b)])
                    P.op("dve", lambda: dve.tensor_scalar(out=qwf[v][0:96, :], in0=PS[b][0:96, :],
                                                          scalar1=qg_t[:, 0:1], scalar2=None, op0=ALU.mult),
                         reads=[pk(b), "qg"], writes=[("qwf", v)])
                    P.op("act", lambda: act.activation(out=qwb[v][:], in_=PS[b][0:96, :], func=AF.Copy,
                                                       scale=qg_t[:, 0:1]),
                         reads=[pk(b), "qg"], writes=[("qwb", v)])
                    P.op("act", lambda: act.activation(out=sq2[v][0:96, :], in_=PS[b][0:96, :], func=AF.Square),
                         reads=[pk(b)], writes=[("sq2", v)])

                def q2(h):
                    v = h % 2
                    b2 = nb()
                    b3 = nb()
                    P.op("pe", lambda: pe.matmul(PS[b2][0:96, :], BO96, sq2[v][0:96, :], start=True, stop=True),
                         reads=[("sq2", v), "cm"], writes=[pk(b2)])
                    P.op("pe", lambda: pe.matmul(PS[b3][0:96, :], R96T, qwb[v][:], start=True, stop=True),
                         reads=[("qwb", v), "cm"], writes=[pk(b3)])
                    P.op("act", lambda: act.activation(out=sdq[0:96, :], in_=PS[b2][0:96, :], func=AF.Ln,
                                                       bias=eps_t[0:96, 0:1], scale=1.0),
                         reads=[pk(b2), "eps"], writes=["sdq"])
                    P.op("act", lambda: act.activation(out=rq[0:96, :], in_=sdq[0:96, :], func=AF.Exp, scale=-0.5),
                         reads=["sdq"], writes=["rq"])
                    P.op("dve", lambda: dve.tensor_tensor(out=qwf[v][0:96, :], in0=qwf[v][0:96, :], in1=cst[tv][:],
                                                          op=ALU.mult),
                         reads=[("qwf", v), ("cst", tv)], writes=[("qwf", v)])
                    P.op("dve", lambda: dve.tensor_tensor(out=qt2[v][:], in0=snt[tv][:], in1=PS[b3][0:96, :],
                                                          op=ALU.mult),
                         reads=[pk(b3), ("snt", tv)], writes=[("qt2", v)])
                    P.op("dve", lambda: dve.tensor_tensor(out=qwf[v][0:96, :], in0=qwf[v][0:96, :], in1=qt2[v][:],
                                                          op=ALU.add),
                         reads=[("qwf", v), ("qt2", v)], writes=[("qwf", v)])
                    P.op("dve", lambda: dve.tensor_tensor(out=qfb[v][0:96, :], in0=qwf[v][0:96, :], in1=rq[0:96, :],
                                                          op=ALU.mult),
                         reads=[("qwf", v), "rq"], writes=[("qfb", v)])
                    P.dma("sp", q_s[h][:, tsl], qfb[v][0:96, :], reads=[("qfb", v)], writes=[("qd", h, i)])

                def k1(hp):
                    v = hp % 2
                    b = qb[("k", hp)] = nb()
                    P.op("pe", lambda: pe.matmul(PS[b][:], wukvt[:, hp * 128:(hp + 1) * 128], ckvn[tv][:],
                                                 start=True, stop=True),
                         reads=["wukv", ("ckvn", tv)], writes=[pk(b)])
                    P.op("dve", lambda: dve.tensor_scalar(out=qwf[v][:], in0=PS[b][:], scalar1=kg_t[:, 0:1],
                                                          scalar2=None, op0=ALU.mult),
                         reads=[pk(b), "kg"], writes=[("qwf", v)])
                    P.op("act", lambda: act.activation(out=sq2[v][:], in_=PS[b][:], func=AF.Square),
                         reads=[pk(b)], writes=[("sq2", v)])

                def k2(hp):
                    v = hp % 2
                    b2 = nb()
                    P.op("pe", lambda: pe.matmul(PS[b2][:], BO128, sq2[v][:], start=True, stop=True),
                         reads=[("sq2", v), "cm"], writes=[pk(b2)])
                    P.op("act", lambda: act.activation(out=sdq[:], in_=PS[b2][:], func=AF.Ln,
                                                       bias=eps_t[:, 0:1], scale=1.0),
                         reads=[pk(b2), "eps"], writes=["sdq"])
                    P.op("act", lambda: act.activation(out=rq[:], in_=sdq[:], func=AF.Exp, scale=-0.5),
                         reads=["sdq"], writes=["rq"])
                    P.op("dve", lambda: dve.tensor_tensor(out=qfb[v][:], in0=qwf[v][:], in1=rq[:], op=ALU.mult),
                         reads=[("qwf", v), "rq"], writes=[("qfb", v)])
                    P.dma("sp", kn_s[hp][:, tsl], qfb[v][:], reads=[("qfb", v)], writes=[("knd", hp, i)])

                def vstep(sub):
                    v = sub % 2
                    b = nb()
                    P.op("pe", lambda: pe.matmul(PS[b][:], ckvn[tv][:, sub * 128:(sub + 1) * 128],
                                                 wukvt[:, 512:1024], start=True, stop=True),
                         reads=["wukv", ("ckvn", tv)], writes=[pk(b)])
                    P.op("act", lambda: act.activation(
                        out=vb[v][:, :, 0:64], in_=PS[b][:].rearrange("p (h d) -> p h d", h=H), func=AF.Copy),
                        reads=[pk(b)], writes=[("vb", v)])
                    P.dma("sp", v_s[i * 4 + sub], vb[v][:].rearrange("p h d -> p (h d)"), reads=[("vb", v)],
                          writes=[("vd", i, sub)])

                seq = [("q", h) for h in range(H)] + [("k", hp) for hp in range(4)]
                f1 = {"q": q1, "k": k1}
                f2 = {"q": q2, "k": k2}
                for n in range(len(seq) + 1):
                    if n < len(seq):
                        kind, a_ = seq[n]
                        steps.append(lambda kind=kind, a_=a_: f1[kind](a_))
                    if n >= 1:
                        kind, a_ = seq[n - 1]
                        steps.append(lambda kind=kind, a_=a_: f2[kind](a_))
                    if n in (2, 5, 8, 11):
                        sub = (2, 5, 8, 11).index(n)
                        steps.append(lambda sub=sub: vstep(sub))
                return steps

            def interleave(a, b_):
                out = []
                na, nb_ = len(a), len(b_)
                ia = ib = 0
                while ia < na or ib < nb_:
                    if ib >= nb_ or (ia < na and ia * nb_ <= ib * na):
                        out.append(a[ia]); ia += 1
                    else:
                        out.append(b_[ib]); ib += 1
                return out

            N_step(0)
            for st_ in X_steps(0):
                st_()
            for i in range(NT):
                ys = Y_steps(i)
                xs_ = X_steps(i + 1) if i + 1 < NT else []
                for st_ in interleave(ys, xs_):
                    st_()
        P.barrier()

    def b2_phase():
        sc = 96.0 ** -0.5
        with ExitStack() as ph:
            vt = sb(ph, "vt", [128, KT, H * 65], BF16)
            kts = [sb(ph, "kts%d" % i, [96, S], BF16) for i in range(2)]
            qts = [sb(ph, "qts%d" % i, [96, TT], BF16) for i in range(2)]
            pb = [sb(ph, "pb%d" % i, [128, TT], BF16) for i in range(4)]
            _osb = sb(ph, "osb0", [64, TT], F32); osb = [_osb, _osb]
            _rrow = sb(ph, "rrow0", [128, TT], F32); rrow = [_rrow, _rrow]
            _onb = sb(ph, "onb0", [64, TT], BF16); onb = [_onb, _onb]
            rhi = sb(ph, "rhi", [128, TT], BF16)
            rlo = sb(ph, "rlo", [128, TT], BF16)
            VG = min(8, KT)
            for k0 in range(0, KT, VG):
                P.dma("sp", vt[:, k0:k0 + VG, :], v_s[k0:k0 + VG].rearrange("k p f -> p k f"), writes=[("v", k0)])
            LOOK = 2
            events = []
            qcnt = 0
            ocnt = 0
            scnt = 0
            for h in range(H):
                ks = h % 2
                for i in range(NT):
                    qs = qcnt % 2
                    qcnt += 1
                    ob = 4 + (ocnt % 2)
                    ov = ocnt % 2
                    ocnt += 1
                    nk = 4 * i + 4
                    for kt in range(nk):
                        events.append(dict(h=h, i=i, kt=kt, nk=nk, ks=ks, qs=qs, ob=ob, ov=ov, sbk=scnt % 4))
                        scnt += 1

            def s_stage(e):
                h, i, kt, ks, qs, sbk = e["h"], e["i"], e["kt"], e["ks"], e["qs"], e["sbk"]
                if kt == 0:
                    if i == 0:
                        P.dma("sp", kts[ks][0:64, :], kn_s[h // 2][(h % 2) * 64:(h % 2) * 64 + 64, :],
                              writes=[("k", ks)])
                        P.dma("sp", kts[ks][64:96, :], kr_s[:, :], writes=[("k", ks)])
                    P.dma("sp", qts[qs][:], q_s[h][:, i * TT:(i + 1) * TT], writes=[("q", qs)])
                j = kt - 4 * i
                col0 = max(0, j) * 128
                e["col0"] = col0
                P.op("pe", lambda: pe.matmul(
                    PS[sbk][:, col0:TT], kts[ks][:, kt * 128:(kt + 1) * 128], qts[qs][:, col0:TT],
                    start=True, stop=True), reads=[("k", ks), ("q", qs)], writes=[pk(sbk)],
                    cost=(TT - col0) / 2.0 + 10.0)
                P.op("act", lambda: act.activation(
                    out=pb[sbk][:, col0:TT], in_=PS[sbk][:, col0:TT], func=AF.Exp, scale=sc),
                    reads=[pk(sbk)], writes=[("pb", sbk)], cost=(TT - col0 + 200) / 1.2)
                if j >= 0:
                    P.op("dve", lambda: dve.tensor_tensor(
                        out=pb[sbk][:, col0:col0 + 128], in0=pb[sbk][:, col0:col0 + 128], in1=TRI, op=ALU.mult),
                        reads=[("pb", sbk), "cm"], writes=[("pb", sbk)], cost=250.0)

            def pv_stage(e):
                h, kt, nk, sbk, ob, col0 = e["h"], e["kt"], e["nk"], e["sbk"], e["ob"], e["col0"]
                P.op("pe", lambda: pe.matmul(
                    PS[ob][0:65, col0:TT], vt[:, kt, h * 65:(h + 1) * 65], pb[sbk][:, col0:TT],
                    start=(kt == 0), stop=(kt == nk - 1)),
                    reads=[("v", (kt // VG) * VG), ("pb", sbk)], writes=[pk(ob)], cost=(TT - col0) / 2.0 + 10.0)

            def fin1(e):
                ob, ov = e["ob"], e["ov"]
                P.op("dve", lambda: dve.reciprocal(out=rrow[ov][64:65, :], in_=PS[ob][64:65, :]),
                     reads=[pk(ob)], writes=["rrow"], cost=3500.0)
                P.op("dve", lambda: dve.tensor_copy(out=osb[ov][:], in_=PS[ob][0:64, :]),
                     reads=[pk(ob)], writes=["osb"], cost=700.0)

            def fin2(e):
                h, i, ov = e["h"], e["i"], e["ov"]
                P.op("dve", lambda: dve.tensor_copy(out=rhi[64:65, :], in_=rrow[ov][64:65, :]),
                     reads=["rrow"], writes=["rhi"], cost=300.0)
                P.op("dve", lambda: dve.tensor_tensor(out=rlo[64:65, :], in0=rrow[ov][64:65, :], in1=rhi[64:65, :],
                                                      op=ALU.subtract),
                     reads=["rrow", "rhi"], writes=["rlo"], cost=300.0)
                P.op("pe", lambda: pe.matmul(PS[6][0:64, :], cm[64:65, 0:64], rhi[64:65, :], start=True, stop=False),
                     reads=["rhi", "cm"], writes=[pk(6)], cost=260.0)
                P.op("pe", lambda: pe.matmul(PS[6][0:64, :], cm[64:65, 0:64], rlo[64:65, :], start=False, stop=True),
                     reads=["rlo", "cm"], writes=[pk(6)], cost=260.0)
                P.op("dve", lambda: dve.tensor_tensor(out=onb[ov][:], in0=osb[ov][:], in1=PS[6][0:64, :],
                                                      op=ALU.mult),
                     reads=["osb", pk(6)], writes=["onb"])
                P.dma("sp", o_s[h][:, i * TT:(i + 1) * TT], onb[ov][:], reads=["onb"], writes=[("od", h, i)])

            deferred = []
            NE = len(events)
            for n in range(NE + LOOK + 3):
                if n < NE:
                    s_stage(events[n])
                m = n - LOOK
                if 0 <= m < NE:
                    e = events[m]
                    pv_stage(e)
                    if e["kt"] == e["nk"] - 1:
                        fin1(e)
                        deferred.append((n + 2, e))
                while deferred and deferred[0][0] <= n:
                    fin2(deferred.pop(0)[1])
            assert not deferred
        P.barrier()

    def c1_phase():
        with ExitStack() as ph:
            wb0t = sb(ph, "wb0t", [128, 4, D], BF16)
            woutt = sb(ph, "woutt", [128, 8, D], BF16)
            ots = [sb(ph, "ots%d" % i, [128, 4, TT], BF16) for i in range(2)]
            g0t = [sb(ph, "g0t%d" % i, [128, 8, TT], BF16) for i in range(2)]
            yct = [sb(ph, "yct%d" % i, [128, 8, TT], BF16) for i in range(2)]
            xts = [sb(ph, "c1x%d" % i, [128, 8, TT], F32) for i in range(2)]
            mg = sb(ph, "mg", [128, 8, TT], BF16)
            tmpf = [sb(ph, "c1t%d" % i, [128, TT], F32) for i in range(2)]
            P.dma("pool", wb0t[:], w_br[0].rearrange("(c p) n -> p c n", p=128), writes=["wb0"])
            P.dma("pool", woutt[:], w_out.rearrange("(c p) n -> p c n", p=128), writes=["wout"])
            x1v = x1s.rearrange("(c p) s -> p c s", p=128)
            x2v = x2s.rearrange("(c p) s -> p c s", p=128)
            g0v = g0_s.rearrange("(c p) s -> p c s", p=128)
            ycv = yc_s.rearrange("(c p) s -> p c s", p=128)
            osv = o_s.rearrange("(c two) d s -> two d c s", two=2)

            def loads(i):
                s_ = i % 2
                tsl = slice(i * TT, (i + 1) * TT)
                for two in range(2):
                    P.dma("sp", ots[s_][two * 64:(two + 1) * 64, :, :], osv[two][:, :, tsl], writes=[("ots", s_)])
                P.dma("sp", g0t[s_][:], g0v[:, :, tsl], writes=[("g0t", s_)])
                P.op("act", lambda s_=s_: act.activation(out=g0t[s_][:], in_=g0t[s_][:], func=AF.Sigmoid),
                     reads=[("g0t", s_)], writes=[("g0t", s_)], cost=3700.0)
                P.dma("sp", yct[s_][:], ycv[:, :, tsl], writes=[("yct", s_)])
                P.dma("sp", xts[s_][:], x1v[:, :, tsl], writes=[("c1x", s_)])

            loads(0)
            for i in range(NT):
                s_ = i % 2
                tsl = slice(i * TT, (i + 1) * TT)
                if i + 1 < NT:
                    loads(i + 1)
                for m in range(8):
                    b = nb()
                    for h in range(4):
                        P.op("pe", lambda h=h, m=m, b=b, s_=s_: pe.matmul(
                            PS[b][:], wb0t[:, h, m * 128:(m + 1) * 128], ots[s_][:, h, :],
                            start=(h == 0), stop=(h == 3)),
                            reads=["wb0", ("ots", s_)], writes=[pk(b)])
                    v = m % 2
                    P.op("dve", lambda m=m, b=b, s_=s_, v=v: dve.tensor_tensor(
                        out=tmpf[v][:], in0=g0t[s_][:, m, :], in1=PS[b][:], op=ALU.mult),
                        reads=[pk(b), ("g0t", s_)], writes=[("c1t", v)])
                    P.op("dve", lambda m=m, s_=s_, v=v: dve.tensor_tensor(
                        out=mg[:, m, :], in0=tmpf[v][:], in1=yct[s_][:, m, :], op=ALU.add),
                        reads=[("c1t", v), ("yct", s_)], writes=[("mg", m)])
                for m2 in range(8):
                    b = nb()
                    for m in range(8):
                        P.op("pe", lambda m=m, m2=m2, b=b: pe.matmul(
                            PS[b][:], woutt[:, m, m2 * 128:(m2 + 1) * 128], mg[:, m, :],
                            start=(m == 0), stop=(m == 7)),
                            reads=["wout", ("mg", m)], writes=[pk(b)])
                    P.op("dve", lambda m2=m2, b=b, s_=s_: dve.scalar_tensor_tensor(
                        out=xts[s_][:, m2, :], in0=PS[b][:], scalar=Gt[:, 8 + m2:8 + m2 + 1],
                        in1=xts[s_][:, m2, :], op0=ALU.mult, op1=ALU.add),
                        reads=[pk(b), ("c1x", s_), ("Gt", 1)], writes=[("c1x", s_)])
                P.dma("sp", x2v[:, :, tsl], xts[s_][:], reads=[("c1x", s_)], writes=[("x2d", i)])
        P.barrier()

    stop_after = getattr(build, "stop_after", None)
    if stop_after == "A":
        ffn_phase(xT, outT, 0, 0, "fa", pre=pre_ffn1)
        wst.close()
    else:
        ffn_phase(xT, x1s, 0, 0, "fa", pre=pre_ffn1)
        wst.close()
        b1_phase()
        if stop_after != "B1":
            wst2 = ExitStack()
            w13_2 = ffn_w13(wst2, 1, "fc")
            b2_phase()
            if stop_after != "B2":
                c1_phase()
                if stop_after != "C1":
                    ffn_phase(x2s, outT, 1, 2, "fc", pre=(w13_2, None))
            wst2.close()

    P.emit()
    build.est_ns = P.est_ns
    st.close()
    return nc


def host_inputs(inputs, S):
    f32 = np.float32
    x = np.asarray(inputs["x"], f32)
    B = x.shape[0]
    g = lambda k: np.asarray(inputs[k])
    inv = (1.0 / (10000.0 ** (np.arange(0, 32, 2, dtype=f32) / f32(32)))).astype(f32)
    invf = np.zeros((96, 1), f32)
    invf[64:80, 0] = inv
    invf[80:96, 0] = inv
    cm = np.zeros((128, 896), f32)
    cm[:, 768:896] = np.eye(128, dtype=f32)
    cm[:, 0:128] = 1.0
    cm[0:64, 128:128 + 64] = 1.0 / 64
    cm[64:96, 128 + 64:128 + 96] = 1.0 / 32
    cm[0:64, 256:256 + 64] = 1.0 / 64
    cm[64:128, 256 + 64:256 + 128] = 1.0 / 64
    for i in range(16):
        cm[80 + i, 384 + 64 + i] = -1.0
        cm[64 + i, 384 + 80 + i] = 1.0
        cm[16 + i, 512 + i] = -1.0
        cm[i, 512 + 16 + i] = 1.0
    cm[0:32, 512 + 32:512 + 64] = 1.0 / 32
    kk = np.arange(128)
    cm[:, 640:768] = (kk[None, :] >= kk[:, None]).astype(f32)
    wukv = g("w_ukv")[0].reshape(128, 8, 2, 64)
    wukv_p = np.concatenate([wukv[:, :, 0, :].reshape(128, 512), wukv[:, :, 1, :].reshape(128, 512)], axis=1)
    shared = {
        "invf": invf,
        "w_ada": np.ascontiguousarray(g("w_ada")[0], f32),
        "b_ada": np.ascontiguousarray(g("b_ada")[0].reshape(72, 128).T, f32),
        "normw": np.ascontiguousarray(g("norm_w")[0].reshape(3, 8, 128).transpose(2, 0, 1).reshape(128, 24), f32),
        "w13": np.ascontiguousarray(g("ffn_w13")[0], f32),
        "w2": np.ascontiguousarray(g("ffn_w2")[0], f32),
        "w_in": np.ascontiguousarray(g("w_in")[0], f32),
        "qa": np.ascontiguousarray(g("q_a_norm")[0].reshape(2, 128).T, f32),
        "kva": np.ascontiguousarray(g("kv_a_norm")[0].reshape(128, 1), f32),
        "w_uq": np.ascontiguousarray(g("w_uq")[0], f32),
        "w_ukv": np.ascontiguousarray(wukv_p, f32),
        "qgain": np.concatenate([g("q_norm_nope")[0], g("q_norm_rope")[0]]).reshape(96, 1).astype(f32),
        "kgain": np.concatenate([g("k_norm_nope")[0], g("k_norm_nope")[0]]).reshape(128, 1).astype(f32),
        "krgain": g("k_norm_rope")[0].reshape(32, 1).astype(f32),
        "convw": np.ascontiguousarray(g("conv_w")[0].T.reshape(4, 128, 3).transpose(1, 0, 2).reshape(128, 12), f32),
        "w_br": np.ascontiguousarray(g("w_branch")[0], f32),
        "w_out": np.ascontiguousarray(g("w_out")[0], f32),
        "cmat": cm,
    }
    maps = []
    for b in range(B):
        m = dict(shared)
        m["xT"] = np.ascontiguousarray(x[b, :S].T)
        m["c_in"] = np.ascontiguousarray(np.asarray(inputs["c"], f32)[b].reshape(8, 128).T)
        m["pos_in"] = np.ascontiguousarray(
            np.broadcast_to(np.asarray(inputs["positions"], np.int32)[b, :S][None, :], (96, S)))
        maps.append(m)
    return maps


_NC_CACHE = {}


def kernel(**inputs):
    S = 8192
    B = 8
    if S not in _NC_CACHE:
        _NC_CACHE[S] = build(S)
    nc = _NC_CACHE[S]
    maps = host_inputs(inputs, S)
    res = run_bass_kernel_spmd(nc, maps, core_ids=list(range(B)))
    out = np.empty((B, S, D), np.float32)
    for b in range(B):
        out[b] = np.asarray(res.results[b]["outT"]).T
    return out
```
